# Optimizing a Trainium2 kernel written in Bass

```python
import math
import jax, jax.numpy as jnp
from jax import lax
import numpy as np

D_MODEL = 1024
BATCH = 2
SEQ = 8192
DEPTH = 4

HEAD_DIM = 64
D_ATTN = 512
N_ATTN_HEADS = D_ATTN // HEAD_DIM
D_SGU = 512
N_SGU_GROUPS = 4
SGU_GROUP_DIM = D_SGU // N_SGU_GROUPS
CHUNK = 128
D_MIX = D_ATTN + D_SGU
D_IN = 3 * D_ATTN + 2 * D_SGU
D_FF = 2816
DILATED_PATTERNS = ((128, 1), (512, 4), (2048, 16))
ROPE_THETA = 500000.0
ROPE_DIM = HEAD_DIM // 4
EPS = 1e-6
NEG_INF = -1e30

kernel_name = "hybrid_dilated_attn_sgu_macaron"


def rms_norm(x, g):
    xf = x.astype(jnp.float32)
    y = xf * lax.rsqrt(jnp.mean(xf * xf, axis=-1, keepdims=True) + EPS)
    return (y * g.astype(jnp.float32)).astype(x.dtype)


def layer_norm(x, g, b):
    xf = x.astype(jnp.float32)
    mu = jnp.mean(xf, axis=-1, keepdims=True)
    xc = xf - mu
    y = xc * lax.rsqrt(jnp.mean(xc * xc, axis=-1, keepdims=True) + EPS)
    return (y * g.astype(jnp.float32) + b.astype(jnp.float32)).astype(x.dtype)


def swiglu(h, w_gate, w_up, w_down):
    return (jax.nn.silu(h @ w_gate) * (h @ w_up)) @ w_down


def partial_rotary(x, positions):
    half = ROPE_DIM // 2
    inv_freq = ROPE_THETA ** (-jnp.arange(0, ROPE_DIM, 2, dtype=jnp.float32) / ROPE_DIM)
    ang = positions.astype(jnp.float32)[:, None] * inv_freq[None, :]
    cos, sin = jnp.cos(ang), jnp.sin(ang)
    xr = x[..., :ROPE_DIM].astype(jnp.float32)
    x1, x2 = xr[..., :half], xr[..., half:]
    rot = jnp.concatenate([x1 * cos - x2 * sin, x1 * sin + x2 * cos], axis=-1)
    return jnp.concatenate([rot.astype(x.dtype), x[..., ROPE_DIM:]], axis=-1)


def banded_attention(q, k, v, half):
    *lead, L, Dh = q.shape
    blk = half
    nb = -(-L // blk)
    pad = nb * blk - L
    lead_pad = [(0, 0)] * len(lead)

    def blocks(t):
        t = jnp.pad(t, lead_pad + [(0, pad), (0, 0)])
        return t.reshape(*lead, nb, blk, Dh)

    def windows(t):
        tp = jnp.pad(t, lead_pad + [(1, 1), (0, 0), (0, 0)])
        return jnp.concatenate([tp[..., :-2, :, :], tp[..., 1:-1, :, :], tp[..., 2:, :, :]], axis=-2)

    qb = blocks(q)
    kw = windows(blocks(k))
    vw = windows(blocks(v))
    s = jnp.einsum("...nqd,...nkd->...nqk", qb, kw).astype(jnp.float32) * (1.0 / math.sqrt(Dh))
    n = jnp.arange(nb)[:, None, None]
    qi = n * blk + jnp.arange(blk)[None, :, None]
    ki = (n - 1) * blk + jnp.arange(3 * blk)[None, None, :]
    mask = (jnp.abs(ki - qi) <= half) & (ki >= 0) & (ki < L)
    s = jnp.where(mask, s, NEG_INF)
    lse = jax.nn.logsumexp(s, axis=-1)
    p = jnp.exp(s - lse[..., None])
    o = jnp.einsum("...nqk,...nkd->...nqd", p.astype(v.dtype), vw)
    o = o.reshape(*lead, nb * blk, Dh)[..., :L, :]
    lse = lse.reshape(*lead, nb * blk)[..., :L]
    return o, lse


def dilated_attention(q, k, v):
    B, H, S, Dh = q.shape
    outs, lses = [], []
    for window, dil in DILATED_PATTERNS:
        half = window // 2 // dil

        def to_strided(t):
            return t.reshape(B, H, S // dil, dil, Dh).swapaxes(2, 3)

        o, l = banded_attention(to_strided(q), to_strided(k), to_strided(v), half)
        outs.append(o.swapaxes(2, 3).reshape(B, H, S, Dh))
        lses.append(l.swapaxes(2, 3).reshape(B, H, S))
    w = jax.nn.softmax(jnp.stack(lses, axis=0), axis=0)
    out = jnp.einsum("pbhs,pbhsd->bhsd", w, jnp.stack(outs, axis=0).astype(jnp.float32))
    return out.astype(q.dtype)


def spatial_gating(u, v, ln_g, ln_b, w_s, b_s):
    B, S, _ = v.shape
    u = jax.nn.gelu(u, approximate=False)
    v = layer_norm(jax.nn.gelu(v, approximate=False), ln_g, ln_b)
    vc = v.reshape(B, S // CHUNK, CHUNK, N_SGU_GROUPS, SGU_GROUP_DIM)
    mixed = jnp.einsum("gts,bcsge->bctge", w_s, vc) + b_s.T[None, None, :, :, None]
    return u * mixed.reshape(B, S, D_SGU)


def setup_inputs(seed: int = 0) -> dict:
    key = jax.random.key(seed)
    ks = jax.random.split(key, 20)
    f32 = jnp.float32

    def nrm(k, shape, scale):
        return jax.random.normal(k, shape, f32) * scale

    def gain(k, shape):
        return 1.0 + 0.02 * jax.random.normal(k, shape, f32)

    L = DEPTH
    return {
        "x": jax.random.normal(ks[0], (BATCH, SEQ, D_MODEL), f32),
        "norm_ffn1": gain(ks[1], (L, D_MODEL)),
        "ffn1_w_gate": nrm(ks[2], (L, D_MODEL, D_FF), D_MODEL ** -0.5),
        "ffn1_w_up": nrm(ks[3], (L, D_MODEL, D_FF), D_MODEL ** -0.5),
        "ffn1_w_down": nrm(ks[4], (L, D_FF, D_MODEL), D_FF ** -0.5),
        "norm_mix": gain(ks[5], (L, D_MODEL)),
        "w_in": nrm(ks[6], (L, D_MODEL, D_IN), D_MODEL ** -0.5),
        "sgu_ln_g": gain(ks[7], (L, D_SGU)),
        "sgu_ln_b": nrm(ks[8], (L, D_SGU), 0.02),
        "sgu_w": nrm(ks[9], (L, N_SGU_GROUPS, CHUNK, CHUNK), 0.5 * CHUNK ** -0.5),
        "sgu_b": gain(ks[10], (L, N_SGU_GROUPS, CHUNK)),
        "out_norm_attn": gain(ks[11], (L, D_ATTN)),
        "out_norm_sgu": gain(ks[12], (L, D_SGU)),
        "w_out": nrm(ks[13], (L, D_MIX, D_MODEL), D_MIX ** -0.5),
        "norm_ffn2": gain(ks[14], (L, D_MODEL)),
        "ffn2_w_gate": nrm(ks[15], (L, D_MODEL, D_FF), D_MODEL ** -0.5),
        "ffn2_w_up": nrm(ks[16], (L, D_MODEL, D_FF), D_MODEL ** -0.5),
        "ffn2_w_down": nrm(ks[17], (L, D_FF, D_MODEL), D_FF ** -0.5),
        "final_norm": gain(ks[18], (D_MODEL,)),
    }


def reference(x, norm_ffn1, ffn1_w_gate, ffn1_w_up, ffn1_w_down, norm_mix, w_in,
              sgu_ln_g, sgu_ln_b, sgu_w, sgu_b, out_norm_attn, out_norm_sgu, w_out,
              norm_ffn2, ffn2_w_gate, ffn2_w_up, ffn2_w_down, final_norm):
    B, S, _ = x.shape
    positions = jnp.arange(S, dtype=jnp.int32)
    splits = [D_ATTN, 2 * D_ATTN, 3 * D_ATTN, 3 * D_ATTN + D_SGU]

    def heads(t):
        return t.reshape(B, S, N_ATTN_HEADS, HEAD_DIM).transpose(0, 2, 1, 3)

    for l in range(DEPTH):
        x = x + 0.5 * swiglu(rms_norm(x, norm_ffn1[l]), ffn1_w_gate[l], ffn1_w_up[l], ffn1_w_down[l])

        h = rms_norm(x, norm_mix[l])
        proj = h @ w_in[l]
        q, k, v, u, g = jnp.split(proj, splits, axis=-1)
        q = partial_rotary(heads(q), positions)
        k = partial_rotary(heads(k), positions)
        a = dilated_attention(q, k, heads(v))
        a = a.transpose(0, 2, 1, 3).reshape(B, S, D_ATTN)
        sg = spatial_gating(u, g, sgu_ln_g[l], sgu_ln_b[l], sgu_w[l], sgu_b[l])
        mixed = jnp.concatenate([rms_norm(a, out_norm_attn[l]), rms_norm(sg, out_norm_sgu[l])], axis=-1)
        x = x + mixed @ w_out[l]

        x = x + 0.5 * swiglu(rms_norm(x, norm_ffn2[l]), ffn2_w_gate[l], ffn2_w_up[l], ffn2_w_down[l])

    return rms_norm(x, final_norm)
```

```python
import numpy as np
import concourse.bass as bass
import concourse.mybir as mybir
from concourse.bass_utils import run_bass_kernel_spmd

F32 = mybir.dt.float32
AF = mybir.ActivationFunctionType
ALU = mybir.AluOpType

D = 1024
NCH = 8
DFF = 2816
NM = 22
TOK = 2048
BLK = 512
NB = TOK // BLK
DEPTH = 4
EPS = 1e-6
NCORES = 8
HALO = 1024
KTOK = TOK + 2 * HALO
PATTERNS = (1, 4, 16)


class Buf:
    __slots__ = ("name", "ap", "w", "r", "multi", "ws")

    def __init__(self, name, ap=None, multi=False):
        self.name = name
        self.ap = ap
        self.w = None
        self.r = []
        self.multi = multi
        self.ws = []


class Sched:
    ENGS = ("pe", "act", "dve", "pool", "sp")
    DMA_RING = 8

    def __init__(self, nc):
        self.nc = nc
        self.ops = {e: [] for e in self.ENGS}
        self.fence = {e: set() for e in self.ENGS}
        self.dma_rr = {}
        self.dma_count = {}
        self.dma_last = {}

    def _deps(self, eng, reads, writes):
        deps = set(self.fence[eng])
        self.fence[eng] = set()
        for b in reads:
            if b.multi:
                deps.update(b.ws)
            elif b.w is not None:
                deps.add(b.w)
        for b in writes:
            if b.multi:
                if b.r:
                    deps.update(b.ws)
            elif b.w is not None:
                deps.add(b.w)
            deps.update(b.r)
        out = set()
        for t in deps:
            if t[0] == "c":
                if t[1] == eng and eng == "pe":
                    continue
                self.ops[t[1]][t[2]]["sig"] = True
            out.add(t)
        return out

    def _mark(self, tok, reads, writes):
        for b in reads:
            b.r.append(tok)
        for b in writes:
            if b.multi:
                if b.r:
                    b.ws = []
                    b.r = []
                b.ws.append(tok)
            else:
                b.w = tok
                b.r = []

    def op(self, eng, fn, reads=(), writes=()):
        deps = self._deps(eng, reads, writes)
        idx = len(self.ops[eng])
        self.ops[eng].append({"fn": fn, "deps": deps, "sig": False, "dma": None})
        tok = ("c", eng, idx)
        self._mark(tok, reads, writes)
        return tok

    def dma(self, queue, semkey, out_ap, in_ap, reads=(), writes=()):
        deps = self._deps(queue, reads, writes)
        i = self.dma_rr.get(semkey, 0)
        self.dma_rr[semkey] = i + 1
        semkey = f"{semkey}#{i % self.DMA_RING}"
        if semkey in self.dma_last:
            deps.add(self.dma_last[semkey])
        n = self.dma_count.get(semkey, 0) + 1
        self.dma_count[semkey] = n
        tok = ("d", semkey, n * 16)
        self.dma_last[semkey] = tok
        self.ops[queue].append({"fn": None, "deps": deps, "sig": False,
                                "dma": (semkey, out_ap, in_ap)})
        self._mark(tok, reads, writes)
        return tok

    def coll(self, semkey, fn, reads=(), writes=()):
        deps = self._deps("pool", reads, writes)
        n = self.dma_count.get(semkey, 0) + 1
        self.dma_count[semkey] = n
        tok = ("d", semkey, n)
        self.dma_last[semkey] = tok
        self.ops["pool"].append({"fn": None, "deps": deps, "sig": False, "dma": None, "coll": (semkey, fn)})
        self._mark(tok, reads, writes)
        return tok

    def barrier(self):
        toks = set(self.dma_last.values())
        for e in self.ENGS:
            if self.ops[e]:
                k = len(self.ops[e]) - 1
                while k >= 0 and (self.ops[e][k]["dma"] is not None or self.ops[e][k].get("coll") is not None
                                  or self.ops[e][k]["fn"] is None):
                    k -= 1
                if k >= 0:
                    toks.add(("c", e, k))
        for e in self.ENGS:
            for t in toks:
                if t[0] == "c":
                    if t[1] == e:
                        continue
                    self.ops[t[1]][t[2]]["sig"] = True
                self.fence[e].add(t)

    def wait_all_dma_on(self, eng):
        for t in self.dma_last.values():
            self.fence[eng].add(t)
        self.op(eng, None)

    def emit(self, block, sems):
        nc = self.nc
        sigval = {}
        for e in self.ENGS:
            c = 0
            vals = []
            for o in self.ops[e]:
                if o["sig"]:
                    c += 1
                vals.append(c)
            sigval[e] = vals
        engobj = {"pe": "tensor", "act": "scalar", "dve": "vector", "pool": "gpsimd", "sp": "sync"}

        def emit_engine(ename, e):
            waited = {}
            for k, o in enumerate(self.ops[ename]):
                for t in sorted(o["deps"]):
                    if t[0] == "c":
                        key = "eng_" + t[1]
                        val = sigval[t[1]][t[2]]
                    else:
                        key = t[1]
                        val = t[2]
                    if waited.get(key, 0) >= val:
                        continue
                    waited[key] = val
                    e.wait_ge(sems[key], val)
                if o.get("coll") is not None:
                    semkey, cfn = o["coll"]
                    cfn(e).then_inc(sems[semkey])
                elif o["dma"] is not None:
                    semkey, out_ap, in_ap = o["dma"]
                    if callable(out_ap):
                        out_ap = out_ap(e)
                    if callable(in_ap):
                        in_ap = in_ap(e)
                    e.dma_start(out=out_ap, in_=in_ap).then_inc(sems[semkey], 16)
                elif o["fn"] is not None:
                    ins = o["fn"](e)
                    if o["sig"]:
                        ins.then_inc(sems["eng_" + ename], 1)
                else:
                    assert not o["sig"]

        for ename in self.ENGS:
            if not self.ops[ename]:
                continue
            getattr(block, engobj[ename])(lambda e, ename=ename: emit_engine(ename, e))


class Prog:
    def __init__(self, steps, layers, dbg=None):
        self.steps = steps
        self.layers = layers
        self.dbg = dbg or {}

    def declare(self, nc):
        d = {}

        def inp(name, shape):
            d[name] = nc.dram_tensor(name, list(shape), F32, kind="ExternalInput").ap()

        def outp(name, shape):
            d[name] = nc.dram_tensor(name, list(shape), F32, kind="ExternalOutput").ap()

        kinds = [k for k, _ in self.steps]
        inp("x_in", (NCH, 128, TOK))
        outp("x_out", (NCH, 128, TOK))
        inp("ones", (128, 128))
        for l in self.layers:
            for w in (1, 2):
                if (f"ffn{w}", l) in self.steps:
                    inp(f"wg{w}_{l}", (NM, 128, NCH, 128))
                    inp(f"wu{w}_{l}", (NM, 128, NCH, 128))
                    inp(f"wd{w}_{l}", (NCH, 128, NM, 128))
                    inp(f"gn{w}_{l}", (128, NCH))
            if ("mixA", l) in self.steps:
                inp(f"wq_{l}", (128, 4, NCH, 128))
                inp(f"wk_{l}", (128, 4, NCH, 128))
                inp(f"wsu_{l}", (128, 4, NCH, 128))
                inp(f"wv_{l}", (128, NCH, 512))
                inp(f"wsg_{l}", (128, NCH, 512))
                inp(f"gnm_{l}", (128, NCH))
                inp(f"gsg_{l}", (128, NCH))
                inp(f"lng_{l}", (512,))
                inp(f"lnb_{l}", (512,))
                inp(f"wsT_{l}", (128, 4, 128))
                inp(f"bsb_{l}", (4, 512))
            if ("mixB", l) in self.steps:
                inp(f"wo_{l}", (128, NCH, NCH, 128))
                inp(f"gat_{l}", (128, NCH))
        def scratch(name, shape):
            return nc.dram_tensor(name, list(shape), F32)

        if "mixA" in kinds:
            inp("rotPT", (128, 128))
            inp("cosT", (128, TOK))
            inp("sinT", (128, TOK))
            inp("masks", (128, 512))
            d["qT"] = scratch("qT_s", (4, 128, TOK)).ap()
            d["sgn"] = scratch("sgn_s", (4, 128, TOK)).ap()
            d["kTh"] = scratch("kTh_s", (4, 128, KTOK)).ap()
            d["vh"] = scratch("vh_s", (KTOK, 512)).ap()
            self.kv_own = {}
            self.kv_all = {}
            for l in self.layers:
                self.kv_own[l] = scratch(f"kv_own{l}", (2 * TOK, 512))
                self.kv_all[l] = scratch(f"kv_all{l}", (NCORES * 2 * TOK, 512))
            if self.dbg.get("a_out"):
                outp("a_out", (4, 128, TOK))
        if self.dbg.get("dump_scratch"):
            outp("qT_dump", (4, 128, TOK))
            outp("sgn_dump", (4, 128, TOK))
        if self.dbg.get("dump_halo"):
            outp("kTh_dump", (4, 128, KTOK))
            outp("vh_dump", (KTOK, 512))
        if "final" in kinds:
            inp("gfinal", (128, NCH))
        self.d = d
        return d

    def build(self):
        from contextlib import ExitStack
        nc = bass.Bass("TRN2", target_bir_lowering=False)
        d = self.declare(nc)
        S = Sched(nc)
        self.S = S
        self.db = {k: Buf("dram_" + k, multi=True) for k in ("qT", "sgn", "kTh", "vh")}
        for l in getattr(self, "kv_own", {}):
            self.db[f"own{l}"] = Buf(f"dram_own{l}", multi=True)
            self.db[f"all{l}"] = Buf(f"dram_all{l}", multi=True)
        with ExitStack() as st:
            def sb(name, cols):
                return st.enter_context(nc.sbuf_tensor("sb_" + name, [128, cols], F32))

            xT = [sb(f"xT{c}", TOK) for c in range(NCH)]
            self.x = [[Buf(f"x{c}_{b}", xT[c][:, b * BLK:(b + 1) * BLK]) for b in range(NB)] for c in range(NCH)]
            hTt = sb("hT", NCH * BLK)
            self.h = [Buf(f"h{c}", hTt[:, c * BLK:(c + 1) * BLK]) for c in range(NCH)]
            ones_t = sb("ones", 128)
            self.ones = Buf("ones", ones_t[:])
            sq_t = sb("sq", 2 * BLK)
            self.sq = [Buf(f"sq{i}", sq_t[:, i * BLK:(i + 1) * BLK]) for i in range(2)]
            rstd_t = sb("rstd", BLK)
            self.rstd = Buf("rstd", rstd_t[:])
            gains_t = sb("gains", 64 * NCH)
            self.gains_t = gains_t
            self.gain_slot = {}
            self.arena = sb("arena", 24576)
            self.ps = []
            for i in range(8):
                p = st.enter_context(nc.psum_tensor(f"ps{i}", [128, BLK], F32))
                self.ps.append(Buf(f"ps{i}", p[:]))

            S.dma("sp", "ld_misc", ones_t[:], d["ones"], writes=[self.ones])
            for c in range(NCH):
                S.dma("sp", "ld_x", xT[c][:], d["x_in"][c], writes=self.x[c])
            self.load_gains()

            for kind, l in self.steps:
                if kind in ("ffn1", "ffn2"):
                    self.ffn(l, int(kind[-1]))
                elif kind == "mixA":
                    self.mixA1(l)
                    S.barrier()
                    self.exchange_start(l)
                    self.mixA2(l)
                    S.barrier()
                    self.exchange_finish(l)
                    if self.dbg.get("dump_halo"):
                        S.dma("sp", "ld_x", d["kTh_dump"], d["kTh"], reads=[self.db["kTh"]])
                        S.dma("sp", "ld_x", d["vh_dump"], d["vh"], reads=[self.db["vh"]])
                elif kind == "mixB":
                    self.mixB(l)
                elif kind == "final":
                    self.final_norm()
                else:
                    raise ValueError(kind)
                S.barrier()

            for c in range(NCH):
                S.dma("pool", "st_x", d["x_out"][c], xT[c][:], reads=self.x[c])
            S.wait_all_dma_on("pool")

            semkeys = ["eng_" + e for e in Sched.ENGS] + sorted(S.dma_count.keys())
            sems = {k: st.enter_context(nc.semaphore(k)) for k in semkeys}
            with nc.Block() as block:
                S.emit(block, sems)
        return nc

    def load_gains(self):
        S, d = self.S, self.d
        i = 0
        for name in sorted(d.keys()):
            if name.startswith(("gn", "gsg", "gat")) or name == "gfinal":
                ap = self.gains_t[:, i * NCH:(i + 1) * NCH]
                b = Buf(name, ap)
                S.dma("sp", "ld_misc", ap, d[name], writes=[b])
                self.gain_slot[name] = b
                i += 1

    def norm_block(self, b, gain, out_bufs=None, src=None, nfeat=D):
        S = self.S
        src = src if src is not None else [self.x[c][b] for c in range(NCH)]
        out_bufs = out_bufs if out_bufs is not None else self.h
        n = len(src)
        ss = self.ps[6]
        for c in range(n):
            sq = self.sq[c % 2]
            S.op("act", lambda e, o=sq.ap, i=src[c].ap: e.activation(out=o, in_=i, func=AF.Square),
                 reads=[src[c]], writes=[sq])
            S.op("pe", lambda e, o=ss.ap, w=self.ones.ap, i=sq.ap, c=c, n=n:
                 e.matmul(o, w, i, start=(c == 0), stop=(c == n - 1)),
                 reads=[self.ones, sq], writes=[ss])
        S.op("act", lambda e, o=self.rstd.ap, i=ss.ap: e.activation(out=o, in_=i, func=AF.Sqrt,
                                                                     scale=1.0 / nfeat, bias=EPS),
             reads=[ss], writes=[self.rstd])
        S.op("dve", lambda e, o=self.rstd.ap: e.reciprocal(o, o), reads=[self.rstd], writes=[self.rstd])
        for c in range(n):
            S.op("dve", lambda e, o=out_bufs[c].ap, i=src[c].ap, g=gain.ap[:, c:c + 1], r=self.rstd.ap:
                 e.scalar_tensor_tensor(o, i, g, r, ALU.mult, ALU.mult),
                 reads=[src[c], gain, self.rstd], writes=[out_bufs[c]])

    def ffn(self, l, w):
        S, d = self.S, self.d
        A = self.arena
        off = 0
        hid = [Buf(f"hid{m}", A[:, off + m * BLK: off + (m + 1) * BLK]) for m in range(NM)]
        off += NM * BLK
        NGU, NDN = 3, 2
        wgu = []
        for s in range(NGU):
            wgu.append((Buf(f"wg_s{s}", A[:, off:off + NCH * 128]),
                        Buf(f"wu_s{s}", A[:, off + NCH * 128: off + 2 * NCH * 128])))
            off += 2 * NCH * 128
        wdn = []
        for s in range(NDN):
            wdn.append(Buf(f"wd_s{s}", A[:, off:off + NM * 128]))
            off += NM * 128
        tmp = [Buf(f"silu{s}", A[:, off + s * BLK: off + (s + 1) * BLK]) for s in range(2)]
        off += 2 * BLK
        assert off <= 24576, off
        gain = self.gain_slot[f"gn{w}_{l}"]
        WG, WU, WD = d[f"wg{w}_{l}"], d[f"wu{w}_{l}"], d[f"wd{w}_{l}"]

        def load_gu(g):
            if g >= NB * NM:
                return
            m = g % NM
            bg, bu = wgu[g % NGU]
            S.dma("sp", "ld_w", bg.ap.rearrange("p (k c) -> p k c", k=NCH), WG[m], writes=[bg])
            S.dma("sp", "ld_w", bu.ap.rearrange("p (k c) -> p k c", k=NCH), WU[m], writes=[bu])

        def load_dn(q):
            if q >= NB * NCH:
                return
            dc = q % NCH
            bd = wdn[q % NDN]
            S.dma("sp", "ld_w", bd.ap.rearrange("p (m c) -> p m c", m=NM), WD[dc], writes=[bd])

        for g in range(NGU):
            load_gu(g)
        for q in range(NDN):
            load_dn(q)

        for b in range(NB):
            self.norm_block(b, gain)
            for m in range(NM):
                g = b * NM + m
                bg, bu = wgu[g % NGU]
                pg, pu = self.ps[m % 2], self.ps[2 + m % 2]

                def mm(e, o, wt):
                    ins = None
                    for k in range(NCH):
                        ins = e.matmul(o, wt[:, k * 128:(k + 1) * 128], self.h[k].ap,
                                       start=(k == 0), stop=(k == NCH - 1))
                    return ins
                S.op("pe", lambda e, o=pg.ap, wt=bg.ap: mm(e, o, wt), reads=[bg] + self.h, writes=[pg])
                S.op("pe", lambda e, o=pu.ap, wt=bu.ap: mm(e, o, wt), reads=[bu] + self.h, writes=[pu])
                t = tmp[m % 2]
                S.op("act", lambda e, o=t.ap, i=pg.ap: e.activation(out=o, in_=i, func=AF.Silu),
                     reads=[pg], writes=[t])
                S.op("dve", lambda e, o=hid[m].ap, a=t.ap, bb=pu.ap: e.tensor_tensor(o, a, bb, ALU.mult),
                     reads=[t, pu], writes=[hid[m]])
                load_gu(g + NGU)
            for dc in range(NCH):
                q = b * NCH + dc
                bd = wdn[q % NDN]
                po = self.ps[4 + dc % 2]

                def mmd(e, o, wt):
                    ins = None
                    for m in range(NM):
                        ins = e.matmul(o, wt[:, m * 128:(m + 1) * 128], hid[m].ap,
                                       start=(m == 0), stop=(m == NM - 1))
                    return ins
                S.op("pe", lambda e, o=po.ap, wt=bd.ap: mmd(e, o, wt), reads=[bd] + hid, writes=[po])
                xb = self.x[dc][b]
                S.op("dve", lambda e, o=xb.ap, p=po.ap: e.scalar_tensor_tensor(o, p, 0.5, o, ALU.mult, ALU.add),
                     reads=[po, xb], writes=[xb])
                load_dn(q + NDN)


    def mixA1(self, l):
        S, d = self.S, self.d
        A = self.arena
        off = 0

        def carve(name, cols):
            nonlocal off
            b = Buf(name, A[:, off:off + cols])
            off += cols
            return b
        wq = carve("wq", 4 * NCH * 128)
        wk = carve("wk", 4 * NCH * 128)
        wv = carve("wv", NCH * 512)
        cosT = carve("cosT", TOK)
        sinT = carve("sinT", TOK)
        rotPT = carve("rotPT", 128)
        ksb = [carve(f"ksb{i}", BLK) for i in range(2)]
        t1 = [carve(f"rt1_{i}", BLK) for i in range(2)]
        t2 = [carve(f"rt2_{i}", BLK) for i in range(2)]
        vsb = [carve(f"vsb{i}", 512) for i in range(2)]
        assert off <= 24576, off
        gain = self.gain_slot[f"gnm_{l}"]
        S.dma("sp", "ld_w", wk.ap.rearrange("p (m k c) -> p m k c", m=4, k=NCH), d[f"wk_{l}"], writes=[wk])
        S.dma("sp", "ld_misc", rotPT.ap, d["rotPT"], writes=[rotPT])
        S.dma("sp", "ld_misc", cosT.ap, d["cosT"], writes=[cosT])
        S.dma("sp", "ld_misc", sinT.ap, d["sinT"], writes=[sinT])
        S.dma("sp", "ld_w", wq.ap.rearrange("p (m k c) -> p m k c", m=4, k=NCH), d[f"wq_{l}"], writes=[wq])
        S.dma("sp", "ld_w", wv.ap.rearrange("p (k n) -> p k n", k=NCH), d[f"wv_{l}"], writes=[wv])
        cnt = 0
        for b in range(NB):
            self.norm_block(b, gain)
            bs = slice(b * BLK, (b + 1) * BLK)
            kvK = self.kv_own[l].ap()[TOK:2 * TOK, :].rearrange("(h j p r) c -> h j p r c", h=2, j=4, p=128, r=2)
            kvV = self.kv_own[l].ap()[0:TOK, :]
            for (wt, isk) in ((wk, True), (wq, False)):
                for mc in range(4):
                    pp = self.ps[cnt % 2]
                    pr = self.ps[2 + cnt % 2]
                    kb, a1, a2 = ksb[cnt % 2], t1[cnt % 2], t2[cnt % 2]
                    cnt += 1

                    def mm(e, o, w, mc=mc):
                        ins = None
                        for k in range(NCH):
                            c0 = (mc * NCH + k) * 128
                            ins = e.matmul(o, w[:, c0:c0 + 128], self.h[k].ap, start=(k == 0), stop=(k == NCH - 1))
                        return ins
                    S.op("pe", lambda e, o=pp.ap, w=wt.ap, mm=mm: mm(e, o, w), reads=[wt] + self.h, writes=[pp])
                    S.op("act", lambda e, o=kb.ap, i=pp.ap: e.copy(o, i), reads=[pp], writes=[kb])
                    S.op("pe", lambda e, o=pr.ap, w=rotPT.ap, i=kb.ap: e.matmul(o, w, i, start=True, stop=True),
                         reads=[rotPT, kb], writes=[pr])
                    S.op("dve", lambda e, o=a1.ap, i=kb.ap, c=cosT.ap[:, bs]: e.tensor_tensor(o, i, c, ALU.mult),
                         reads=[kb, cosT], writes=[a1])
                    S.op("dve", lambda e, o=a2.ap, i=pr.ap, c=sinT.ap[:, bs]: e.tensor_tensor(o, i, c, ALU.mult),
                         reads=[pr, sinT], writes=[a2])
                    S.op("pool", lambda e, o=a1.ap, i=a2.ap: e.tensor_tensor(o, o, i, ALU.add),
                         reads=[a1, a2], writes=[a1])
                    if isk:
                        S.dma("pool", "st_a", d["kTh"][mc][:, HALO + b * BLK: HALO + (b + 1) * BLK], a1.ap,
                              reads=[a1], writes=[self.db["kTh"]])
                        S.dma("pool", "st_a", kvK[b // 2, mc, :, b % 2, :], a1.ap, reads=[a1], writes=[self.db[f"own{l}"]])
                    else:
                        S.dma("pool", "st_a", d["qT"][mc][:, bs], a1.ap, reads=[a1], writes=[self.db["qT"]])
            for tt in range(4):
                pv = self.ps[4 + tt % 2]
                vb = vsb[tt % 2]
                ts_ = slice(tt * 128, (tt + 1) * 128)

                def mmv(e, o, w, ts_=ts_):
                    ins = None
                    for k in range(NCH):
                        ins = e.matmul(o, self.h[k].ap[:, ts_], w[:, k * 512:(k + 1) * 512],
                                       start=(k == 0), stop=(k == NCH - 1))
                    return ins
                S.op("pe", lambda e, o=pv.ap, w=wv.ap, mmv=mmv: mmv(e, o, w), reads=[wv] + self.h, writes=[pv])
                S.op("act", lambda e, o=vb.ap, i=pv.ap: e.copy(o, i), reads=[pv], writes=[vb])
                r0 = b * BLK + tt * 128
                S.dma("pool", "st_a", d["vh"][HALO + r0:HALO + r0 + 128, :], vb.ap, reads=[vb], writes=[self.db["vh"]])
                S.dma("pool", "st_a", kvV[r0:r0 + 128, :], vb.ap, reads=[vb], writes=[self.db[f"own{l}"]])

    def mixA2(self, l):
        S, d = self.S, self.d
        A = self.arena
        off = 0

        def carve(name, cols):
            nonlocal off
            b = Buf(name, A[:, off:off + cols])
            off += cols
            return b
        wu = carve("wu2", 4 * NCH * 128)
        wg = carve("wg2", NCH * 512)
        lng = carve("lng", 512)
        lnb = carve("lnb", 512)
        bsb = [carve(f"bsb{g}", 512) for g in range(4)]
        wsT = carve("wsT", 4 * 128)
        gel = [carve(f"gel{i}", 512) for i in range(2)]
        vn = [carve(f"vn{i}", 512) for i in range(2)]
        junk = carve("junk", 512)
        st_ = [carve(f"stat{i}", 8) for i in range(2)]
        gu = [carve(f"gu{g}", BLK) for g in range(4)]
        sg = [carve(f"sg{g}", BLK) for g in range(4)]
        sgn = [carve(f"sgn{g}", BLK) for g in range(4)]
        assert off <= 24576, off
        gain = self.gain_slot[f"gnm_{l}"]
        gsg = self.gain_slot[f"gsg_{l}"]
        S.dma("sp", "ld_w", wg.ap.rearrange("p (k n) -> p k n", k=NCH), d[f"wsg_{l}"], writes=[wg])
        S.dma("sp", "ld_misc", lng.ap, d[f"lng_{l}"].partition_broadcast(128), writes=[lng])
        S.dma("sp", "ld_misc", lnb.ap, d[f"lnb_{l}"].partition_broadcast(128), writes=[lnb])
        S.dma("sp", "ld_misc", wsT.ap.rearrange("p (g t) -> p g t", g=4), d[f"wsT_{l}"], writes=[wsT])
        for g in range(4):
            S.dma("sp", "ld_misc", bsb[g].ap, d[f"bsb_{l}"][g].partition_broadcast(128), writes=[bsb[g]])
        S.dma("sp", "ld_w", wu.ap.rearrange("p (m k c) -> p m k c", m=4, k=NCH), d[f"wsu_{l}"], writes=[wu])
        for b in range(NB):
            self.norm_block(b, gain)
            bs = slice(b * BLK, (b + 1) * BLK)
            pmix = self.ps[0:4]
            for tt in range(4):
                pg = self.ps[4 + tt % 2]
                gl, vv, sx = gel[tt % 2], vn[tt % 2], st_[tt % 2]
                ts_ = slice(tt * 128, (tt + 1) * 128)

                def mmg(e, o, w, ts_=ts_):
                    ins = None
                    for k in range(NCH):
                        ins = e.matmul(o, self.h[k].ap[:, ts_], w[:, k * 512:(k + 1) * 512],
                                       start=(k == 0), stop=(k == NCH - 1))
                    return ins
                S.op("pe", lambda e, o=pg.ap, w=wg.ap, mmg=mmg: mmg(e, o, w), reads=[wg] + self.h, writes=[pg])
                S.op("dve", lambda e, a=sx.ap: e.memset(a, 0.0), writes=[sx])
                S.op("act", lambda e, o=gl.ap, i=pg.ap, a=sx.ap[:, 0:1]: e.activation(out=o, in_=i, func=AF.Gelu, accum_out=a),
                     reads=[pg], writes=[gl, sx])
                S.op("dve", lambda e, a=sx.ap: e.tensor_scalar(a[:, 1:2], a[:, 0:1], 1.0 / 512, None, ALU.mult),
                     reads=[sx], writes=[sx])
                S.op("dve", lambda e, o=gl.ap, a=sx.ap: e.tensor_scalar(o, o, a[:, 1:2], None, ALU.subtract),
                     reads=[gl, sx], writes=[gl])
                S.op("act", lambda e, o=junk.ap, i=gl.ap, a=sx.ap[:, 2:3]: e.activation(out=o, in_=i, func=AF.Square, accum_out=a),
                     reads=[gl], writes=[junk, sx])
                S.op("act", lambda e, a=sx.ap: e.activation(out=a[:, 3:4], in_=a[:, 2:3], func=AF.Sqrt, scale=1.0 / 512, bias=EPS),
                     reads=[sx], writes=[sx])
                S.op("dve", lambda e, a=sx.ap: e.reciprocal(a[:, 4:5], a[:, 3:4]), reads=[sx], writes=[sx])
                S.op("dve", lambda e, o=vv.ap, i=gl.ap, a=sx.ap, g_=lng.ap: e.scalar_tensor_tensor(o, i, a[:, 4:5], g_, ALU.mult, ALU.mult),
                     reads=[gl, sx, lng], writes=[vv])
                S.op("dve", lambda e, o=vv.ap, b_=lnb.ap: e.tensor_tensor(o, o, b_, ALU.add),
                     reads=[vv, lnb], writes=[vv])
                for g in range(4):
                    S.op("pe", lambda e, o=pmix[g].ap[:, ts_], w=vv.ap[:, g * 128:(g + 1) * 128], r=wsT.ap[:, g * 128:(g + 1) * 128]:
                         e.matmul(o, w, r, start=True, stop=True),
                         reads=[vv, wsT], writes=[pmix[g]])
            for g in range(4):
                pu = self.ps[4 + g % 2]

                def mmu(e, o, w, g=g):
                    ins = None
                    for k in range(NCH):
                        c0 = (g * NCH + k) * 128
                        ins = e.matmul(o, w[:, c0:c0 + 128], self.h[k].ap, start=(k == 0), stop=(k == NCH - 1))
                    return ins
                S.op("pe", lambda e, o=pu.ap, w=wu.ap, mmu=mmu: mmu(e, o, w), reads=[wu] + self.h, writes=[pu])
                S.op("act", lambda e, o=gu[g].ap, i=pu.ap: e.activation(out=o, in_=i, func=AF.Gelu),
                     reads=[pu], writes=[gu[g]])
                S.op("dve", lambda e, o=sg[g].ap, i=pmix[g].ap, b_=bsb[g].ap: e.tensor_tensor(o, i, b_, ALU.add),
                     reads=[pmix[g], bsb[g]], writes=[sg[g]])
                S.op("dve", lambda e, o=sg[g].ap, i=gu[g].ap: e.tensor_tensor(o, o, i, ALU.mult),
                     reads=[sg[g], gu[g]], writes=[sg[g]])
            self.norm_block(b, gsg, out_bufs=sgn, src=sg, nfeat=512)
            for g in range(4):
                S.dma("pool", "st_a", d["sgn"][g][:, bs], sgn[g].ap, reads=[sgn[g]], writes=[self.db["sgn"]])


    def exchange_start(self, l):
        S = self.S
        own, allg = self.kv_own[l], self.kv_all[l]
        S.coll("cc", lambda e: e.collective_compute("AllGather", ALU.bypass, replica_groups=[list(range(NCORES))],
                                                    ins=[own.ap().opt()], outs=[allg.ap().opt()]),
               reads=[self.db[f"own{l}"]], writes=[self.db[f"all{l}"]])
        self.S.fence["pool"].add(self.S.dma_last["cc"])
        self.S.op("pool", None)

    def exchange_finish(self, l):
        S, d = self.S, self.d
        allg = self.kv_all[l].ap()
        RB = 2 * TOK
        kth, vh = d["kTh"], d["vh"]

        def pid(e):
            if getattr(self, "_pid", None) is None:
                self._pid = e.partition_id()
            return self._pid

        def prev(e):
            return (pid(e) + (NCORES - 1)) % NCORES

        def nxt(e):
            return (pid(e) + 1) % NCORES
        rd, wr = [self.db[f"all{l}"]], None
        S.dma("sp", "ld_x", vh[0:HALO, :], lambda e: allg[bass.ds(prev(e) * RB + HALO, HALO), :],
              reads=rd, writes=[self.db["vh"]])
        S.dma("sp", "ld_x", vh[HALO + TOK:KTOK, :], lambda e: allg[bass.ds(nxt(e) * RB, HALO), :],
              reads=rd, writes=[self.db["vh"]])
        S.dma("sp", "ld_x", kth[:, :, 0:HALO].rearrange("j p (r c) -> j p r c", c=512),
              lambda e: allg[bass.ds(prev(e) * RB + TOK + HALO, HALO), :].rearrange("(j p r) c -> j p r c", j=4, p=128, r=2),
              reads=rd, writes=[self.db["kTh"]])
        S.dma("sp", "ld_x", kth[:, :, HALO + TOK:KTOK].rearrange("j p (r c) -> j p r c", c=512),
              lambda e: allg[bass.ds(nxt(e) * RB + TOK, HALO), :].rearrange("(j p r) c -> j p r c", j=4, p=128, r=2),
              reads=rd, writes=[self.db["kTh"]])

    def mixB(self, l):
        S, d = self.S, self.d
        A = self.arena
        off = 0

        def carve(name, cols):
            nonlocal off
            b = Buf(name, A[:, off:off + cols])
            off += cols
            return b
        a_sb = [carve(f"a_sb{j}", TOK) for j in range(4)]
        off_attn = off
        qT = carve("qT", TOK)
        kT = carve("kTh", KTOK)
        VT = 17 * 128
        vsl = [carve(f"vt{i}", VT) for i in range(2)]
        uz = carve("uzacc", 2 * TOK)
        esb = [carve(f"esb{i}", 256) for i in range(3)]
        masks = carve("masks", 512)
        assert off <= 24576, off
        uz3 = uz.ap.rearrange("p (u t) -> p u t", u=2)
        S.dma("sp", "ld_misc", masks.ap, d["masks"], writes=[masks])
        vcnt = 0
        ecnt = 0
        for j in self.dbg.get("pairs", range(4)):
            S.dma("sp", "ld_a", qT.ap, d["qT"][j], reads=[self.db["qT"]], writes=[qT])
            S.dma("sp", "ld_a", kT.ap, d["kTh"][j], reads=[self.db["kTh"]], writes=[kT])
            pats = self.dbg.get("patterns", PATTERNS)
            for d_ in pats:
                nq = TOK // d_ // 128
                base = HALO - 64 * d_
                for r in range(d_):
                    vt = vsl[vcnt % 2]
                    vcnt += 1
                    nrow = (nq + 1) * 128
                    r0 = base + r
                    src = d["vh"][r0:r0 + d_ * (nrow - 1) + 1:d_, j * 128:(j + 1) * 128]
                    vt3 = vt.ap[:, 0:(nq + 1) * 128].rearrange("p (i c) -> p i c", c=128)
                    S.dma("sp", "ld_a", vt3, src.rearrange("(i p) c -> p i c", p=128), reads=[self.db["vh"]], writes=[vt])
                    for i in range(nq + 1):
                        qlo, qhi = max(i - 1, 0), min(i, nq - 1)
                        N = 128 * (qhi - qlo + 1)
                        if i == 0:
                            mk = masks.ap[:, 256:384]
                        elif i == nq:
                            mk = masks.ap[:, 384:512]
                        else:
                            mk = masks.ap[:, 0:256]
                        k0 = base + r + d_ * 128 * i
                        q0 = r + d_ * 128 * qlo
                        for hh in range(2):
                            R = slice(hh * 64, hh * 64 + 64)
                            pS = self.ps[ecnt % 3]
                            eb = esb[ecnt % 3]
                            ecnt += 1
                            kap = kT.ap[R, k0:k0 + d_ * 127 + 1:d_]
                            qap = qT.ap[R, q0:q0 + d_ * (N - 1) + 1:d_]
                            S.op("pe", lambda e, o=pS.ap[:, 0:N], w=kap, i_=qap: e.matmul(o, w, i_, start=True, stop=True),
                                 reads=[kT, qT], writes=[pS])
                            S.op("act", lambda e, o=eb.ap[:, 0:N], i_=pS.ap[:, 0:N]: e.activation(out=o, in_=i_, func=AF.Exp, scale=0.125),
                                 reads=[pS], writes=[eb])
                            S.op("pool", lambda e, o=eb.ap[:, 0:N], m=mk: e.tensor_tensor(o, o, m, ALU.mult),
                                 reads=[eb, masks], writes=[eb])
                            for qt in range(qlo, qhi + 1):
                                pz = self.ps[4 + qt % 2]
                                ecol = slice((qt - qlo) * 128, (qt - qlo + 1) * 128)
                                vap = vt.ap[:, i * 128 + hh * 64: i * 128 + hh * 64 + 64]
                                tp = (0, 64) if hh == 1 else None

                                def pv(e, pz=pz, R=R, vap=vap, eap=eb.ap[:, ecol], tp=tp, st=(i == qt), sp=(i == qt + 1)):
                                    kw = {"tile_position": tp} if tp is not None else {}
                                    e.matmul(pz.ap[R, 0:128], vap, eap, start=st, stop=sp, **kw)
                                    return e.matmul(pz.ap[R, 128:256], self.ones.ap[:, 0:64], eap, start=False, stop=sp,
                                                    skip_group_check=True, **kw)
                                S.op("pe", pv, reads=[vt, eb, self.ones], writes=[pz])
                        if i >= 1:
                            qt = i - 1
                            pz = self.ps[4 + qt % 2]
                            t0 = r + d_ * 128 * qt
                            dst = uz3[:, :, t0:t0 + d_ * 127 + 1:d_]
                            srcp = pz.ap[:, 0:256].rearrange("p (u t) -> p u t", u=2)
                            if d_ == pats[0]:
                                S.op("dve", lambda e, o=dst, i_=srcp: e.tensor_copy(o, i_), reads=[pz], writes=[uz])
                            else:
                                S.op("dve", lambda e, o=dst, i_=srcp: e.tensor_tensor(o, o, i_, ALU.add), reads=[pz, uz], writes=[uz])
            S.op("dve", lambda e, z=uz.ap[:, TOK:2 * TOK]: e.reciprocal(z, z), reads=[uz], writes=[uz])
            S.op("dve", lambda e, o=a_sb[j].ap, u=uz.ap[:, 0:TOK], z=uz.ap[:, TOK:2 * TOK]: e.tensor_tensor(o, u, z, ALU.mult),
                 reads=[uz], writes=[a_sb[j]])
            if self.dbg.get("a_out"):
                S.dma("pool", "st_a", d["a_out"][j], a_sb[j].ap, reads=[a_sb[j]])
        S.barrier()
        if self.dbg.get("dump_scratch"):
            S.dma("sp", "ld_x", d["qT_dump"], d["qT"], reads=[self.db["qT"]])
            S.dma("sp", "ld_x", d["sgn_dump"], d["sgn"], reads=[self.db["sgn"]])
        if self.dbg.get("no_wout"):
            return
        off = off_attn
        wo = carve("wo", NCH * NCH * 128)
        an = [carve(f"an{j}", BLK) for j in range(4)]
        sgb = [carve(f"sgb{j}", BLK) for j in range(4)]
        assert off <= 24576, off
        gat = self.gain_slot[f"gat_{l}"]
        S.dma("sp", "ld_w", wo.ap.rearrange("p (a k c) -> p a k c", a=NCH, k=NCH), d[f"wo_{l}"], writes=[wo])
        for b in range(NB):
            bs = slice(b * BLK, (b + 1) * BLK)
            for g in range(4):
                S.dma("sp", "ld_a", sgb[g].ap, d["sgn"][g][:, bs], reads=[self.db["sgn"]], writes=[sgb[g]])
            srcs = [Buf(f"a{j}_{b}", a_sb[j].ap[:, bs]) for j in range(4)]
            self.norm_block(b, gat, out_bufs=an, src=srcs, nfeat=512)
            rhs = an + sgb
            for dc in range(NCH):
                po = self.ps[dc % 2]

                def mmo(e, o, dc=dc):
                    ins = None
                    for k in range(NCH):
                        c0 = (dc * NCH + k) * 128
                        ins = e.matmul(o, wo.ap[:, c0:c0 + 128], rhs[k].ap, start=(k == 0), stop=(k == NCH - 1))
                    return ins
                S.op("pe", lambda e, o=po.ap, mmo=mmo: mmo(e, o), reads=[wo] + rhs, writes=[po])
                xb = self.x[dc][b]
                S.op("dve", lambda e, o=xb.ap, p=po.ap: e.tensor_tensor(o, o, p, ALU.add), reads=[po, xb], writes=[xb])

    def final_norm(self):
        gain = self.gain_slot["gfinal"]
        for b in range(NB):
            self.norm_block(b, gain, out_bufs=[self.x[c][b] for c in range(NCH)])


def _feat_major(v, n):
    return np.ascontiguousarray(np.asarray(v, np.float32).reshape(n, 128).T)


def _w_stationary(W):
    K, M = W.shape
    return np.ascontiguousarray(W.reshape(K // 128, 128, M // 128, 128).transpose(2, 1, 0, 3))


def _x_to_cores(x):
    outs = []
    for core in range(NCORES):
        b, q = divmod(core, 4)
        xs = x[b, q * TOK:(q + 1) * TOK, :]
        outs.append(np.ascontiguousarray(xs.T.reshape(NCH, 128, TOK)))
    return outs


def _cores_to_x(xs):
    out = np.empty((2, 8192, D), np.float32)
    for core in range(NCORES):
        b, q = divmod(core, 4)
        out[b, q * TOK:(q + 1) * TOK, :] = xs[core].reshape(D, TOK).T
    return out


def _w_stat_p_major(W):
    K, M = W.shape
    return np.ascontiguousarray(W.reshape(K // 128, 128, M // 128, 128).transpose(1, 2, 0, 3))


def _w_moving(W):
    K, N = W.shape
    return np.ascontiguousarray(W.reshape(K // 128, 128, N).transpose(1, 0, 2))


def _pad8(a):
    out = np.zeros((128, NCH), np.float32)
    out[:, :a.shape[1]] = a
    return out


def _const_tables():
    inv_freq = (np.float32(500000.0) ** (-np.arange(0, 16, 2, dtype=np.float32) / np.float32(16))).astype(np.float32)
    cos_q, sin_q = [], []
    for q in range(4):
        pos = np.arange(q * TOK, (q + 1) * TOK, dtype=np.float32)
        ang = (pos[:, None] * inv_freq[None, :]).astype(np.float32)
        c8, s8 = np.cos(ang).astype(np.float32), np.sin(ang).astype(np.float32)
        C = np.ones((128, TOK), np.float32)
        Sn = np.zeros((128, TOK), np.float32)
        for hb in (0, 64):
            for i in range(8):
                C[hb + i] = c8[:, i]
                C[hb + 8 + i] = c8[:, i]
                Sn[hb + i] = s8[:, i]
                Sn[hb + 8 + i] = s8[:, i]
        cos_q.append(C)
        sin_q.append(Sn)
    rotPT = np.zeros((128, 128), np.float32)
    for hb in (0, 64):
        for i in range(8):
            rotPT[hb + i + 8, hb + i] = -1.0
            rotPT[hb + i, hb + i + 8] = 1.0
    p = np.arange(128)[:, None]
    f = np.arange(128)[None, :]
    Bm = (f >= p).astype(np.float32)
    Am = (f <= p).astype(np.float32)
    masks = []
    for q in range(4):
        has_prev = 1.0 if q > 0 else 0.0
        has_next = 1.0 if q < 3 else 0.0
        first = Am * np.where(p >= 64, 1.0, has_prev)
        last = Bm * np.where(p < 64, 1.0, has_next)
        masks.append(np.ascontiguousarray(np.concatenate([Bm, Am, first, last], axis=1).astype(np.float32)))
    return cos_q, sin_q, rotPT, masks


_PROG_CACHE = {}


def _get_prog(steps):
    key = tuple(steps)
    if key not in _PROG_CACHE:
        P = Prog(list(steps), sorted({l for k, l in steps if k != "final"}))
        _PROG_CACHE[key] = P.build()
    return _PROG_CACHE[key]


def kernel(x, norm_ffn1, ffn1_w_gate, ffn1_w_up, ffn1_w_down, norm_mix, w_in,
           sgu_ln_g, sgu_ln_b, sgu_w, sgu_b, out_norm_attn, out_norm_sgu, w_out,
           norm_ffn2, ffn2_w_gate, ffn2_w_up, ffn2_w_down, final_norm):
    f32 = lambda a: np.asarray(a, np.float32)
    x = f32(x)
    cos_q, sin_q, rotPT, masks = _const_tables()
    ones = np.ones((128, 128), np.float32)

    def layer_inputs(l, slot, kinds):
        m = {}
        if "ffn1" in kinds:
            m[f"wg1_{slot}"] = _w_stationary(f32(ffn1_w_gate[l]))
            m[f"wu1_{slot}"] = _w_stationary(f32(ffn1_w_up[l]))
            m[f"wd1_{slot}"] = _w_stationary(f32(ffn1_w_down[l]))
            m[f"gn1_{slot}"] = _feat_major(norm_ffn1[l], NCH)
        if "ffn2" in kinds:
            m[f"wg2_{slot}"] = _w_stationary(f32(ffn2_w_gate[l]))
            m[f"wu2_{slot}"] = _w_stationary(f32(ffn2_w_up[l]))
            m[f"wd2_{slot}"] = _w_stationary(f32(ffn2_w_down[l]))
            m[f"gn2_{slot}"] = _feat_major(norm_ffn2[l], NCH)
        if "mixA" in kinds:
            wi = f32(w_in[l])
            m[f"wq_{slot}"] = _w_stat_p_major(wi[:, 0:512])
            m[f"wk_{slot}"] = _w_stat_p_major(wi[:, 512:1024])
            m[f"wv_{slot}"] = _w_moving(wi[:, 1024:1536])
            m[f"wsu_{slot}"] = _w_stat_p_major(wi[:, 1536:2048])
            m[f"wsg_{slot}"] = _w_moving(wi[:, 2048:2560])
            m[f"gnm_{slot}"] = _feat_major(norm_mix[l], NCH)
            m[f"gsg_{slot}"] = _pad8(_feat_major(out_norm_sgu[l], 4))
            m[f"lng_{slot}"] = np.ascontiguousarray(f32(sgu_ln_g[l]))
            m[f"lnb_{slot}"] = np.ascontiguousarray(f32(sgu_ln_b[l]))
            m[f"wsT_{slot}"] = np.ascontiguousarray(f32(sgu_w[l]).transpose(2, 0, 1))
            m[f"bsb_{slot}"] = np.ascontiguousarray(np.tile(f32(sgu_b[l]), (1, 4)))
        if "mixB" in kinds:
            m[f"wo_{slot}"] = _w_stat_p_major(f32(w_out[l]))
            m[f"gat_{slot}"] = _pad8(_feat_major(out_norm_attn[l], 4))
        return m

    xs = _x_to_cores(x)
    steps = []
    common = {"ones": ones, "rotPT": rotPT, "gfinal": _feat_major(final_norm, NCH)}
    for l in range(DEPTH):
        steps += [("ffn1", l), ("mixA", l), ("mixB", l), ("ffn2", l)]
        common.update(layer_inputs(l, l, ("ffn1", "mixA", "mixB", "ffn2")))
    steps += [("final", 0)]
    nc = _get_prog(steps)
    in_maps = []
    for core in range(NCORES):
        b, q = divmod(core, 4)
        m = dict(common)
        m["x_in"] = xs[core]
        m["cosT"] = cos_q[q]
        m["sinT"] = sin_q[q]
        m["masks"] = masks[q]
        in_maps.append(m)
    res = run_bass_kernel_spmd(nc, in_maps, core_ids=list(range(NCORES)))
    xs = [np.asarray(r["x_out"]) for r in res.results]
    return _cores_to_x(xs)
```

```python
import numpy as np
import concourse.bass as bass
import concourse.mybir as mybir
from concourse.bass_utils import run_bass_kernel_spmd

F32 = mybir.dt.float32
AF = mybir.ActivationFunctionType
ALU = mybir.AluOpType

D = 1024
NCH = 8
DFF = 2816
NM = 22
TOK = 2048
BLK = 512
NB = TOK // BLK
DEPTH = 4
EPS = 1e-6
NCORES = 8
HALO = 1024
KTOK = TOK + 2 * HALO
ARENA = 26624
PATTERNS = (1, 4, 16)


class Buf:
    __slots__ = ("name", "ap", "w", "r", "multi", "ws")

    def __init__(self, name, ap=None, multi=False):
        self.name = name
        self.ap = ap
        self.w = None
        self.r = []
        self.multi = multi
        self.ws = []


class Sched:
    ENGS = ("pe", "act", "dve", "pool", "sp")
    DMA_RING = 8

    def __init__(self, nc):
        self.nc = nc
        self.ops = {e: [] for e in self.ENGS}
        self.fence = {e: set() for e in self.ENGS}
        self.dma_rr = {}
        self.dma_count = {}
        self.dma_last = {}

    def _deps(self, eng, reads, writes):
        deps = set(self.fence[eng])
        self.fence[eng] = set()
        for b in reads:
            if b.multi:
                deps.update(b.ws)
            elif b.w is not None:
                deps.add(b.w)
        for b in writes:
            if b.multi:
                if b.r:
                    deps.update(b.ws)
            elif b.w is not None:
                deps.add(b.w)
            deps.update(b.r)
        out = set()
        for t in deps:
            if t[0] == "c":
                if t[1] == eng and eng == "pe":
                    continue
                self.ops[t[1]][t[2]]["sig"] = True
            out.add(t)
        return out

    def _mark(self, tok, reads, writes):
        for b in reads:
            b.r.append(tok)
        for b in writes:
            if b.multi:
                if b.r:
                    b.ws = []
                    b.r = []
                b.ws.append(tok)
            else:
                b.w = tok
                b.r = []

    def op(self, eng, fn, reads=(), writes=()):
        deps = self._deps(eng, reads, writes)
        idx = len(self.ops[eng])
        self.ops[eng].append({"fn": fn, "deps": deps, "sig": False, "dma": None})
        tok = ("c", eng, idx)
        self._mark(tok, reads, writes)
        return tok

    def dma(self, queue, semkey, out_ap, in_ap, reads=(), writes=()):
        deps = self._deps(queue, reads, writes)
        i = self.dma_rr.get(semkey, 0)
        self.dma_rr[semkey] = i + 1
        semkey = f"{semkey}#{i % self.DMA_RING}"
        if semkey in self.dma_last:
            deps.add(self.dma_last[semkey])
        n = self.dma_count.get(semkey, 0) + 1
        self.dma_count[semkey] = n
        tok = ("d", semkey, n * 16)
        self.dma_last[semkey] = tok
        self.ops[queue].append({"fn": None, "deps": deps, "sig": False,
                                "dma": (semkey, out_ap, in_ap)})
        self._mark(tok, reads, writes)
        return tok

    def coll(self, semkey, fn, reads=(), writes=()):
        deps = self._deps("pool", reads, writes)
        n = self.dma_count.get(semkey, 0) + 1
        self.dma_count[semkey] = n
        tok = ("d", semkey, n)
        self.dma_last[semkey] = tok
        self.ops["pool"].append({"fn": None, "deps": deps, "sig": False, "dma": None, "coll": (semkey, fn)})
        self._mark(tok, reads, writes)
        return tok

    def barrier(self):
        toks = set(self.dma_last.values())
        for e in self.ENGS:
            if self.ops[e]:
                k = len(self.ops[e]) - 1
                while k >= 0 and (self.ops[e][k]["dma"] is not None or self.ops[e][k].get("coll") is not None
                                  or self.ops[e][k]["fn"] is None):
                    k -= 1
                if k >= 0:
                    toks.add(("c", e, k))
        for e in self.ENGS:
            for t in toks:
                if t[0] == "c":
                    if t[1] == e:
                        continue
                    self.ops[t[1]][t[2]]["sig"] = True
                self.fence[e].add(t)

    def wait_all_dma_on(self, eng):
        for t in self.dma_last.values():
            self.fence[eng].add(t)
        self.op(eng, None)

    def emit(self, block, sems):
        nc = self.nc
        sigval = {}
        for e in self.ENGS:
            c = 0
            vals = []
            for o in self.ops[e]:
                if o["sig"]:
                    c += 1
                vals.append(c)
            sigval[e] = vals
        engobj = {"pe": "tensor", "act": "scalar", "dve": "vector", "pool": "gpsimd", "sp": "sync"}

        def emit_engine(ename, e):
            waited = {}
            for k, o in enumerate(self.ops[ename]):
                for t in sorted(o["deps"]):
                    if t[0] == "c":
                        key = "eng_" + t[1]
                        val = sigval[t[1]][t[2]]
                    else:
                        key = t[1]
                        val = t[2]
                    if waited.get(key, 0) >= val:
                        continue
                    waited[key] = val
                    e.wait_ge(sems[key], val)
                if o.get("coll") is not None:
                    semkey, cfn = o["coll"]
                    cfn(e).then_inc(sems[semkey])
                elif o["dma"] is not None:
                    semkey, out_ap, in_ap = o["dma"]
                    if callable(out_ap):
                        out_ap = out_ap(e)
                    if callable(in_ap):
                        in_ap = in_ap(e)
                    e.dma_start(out=out_ap, in_=in_ap).then_inc(sems[semkey], 16)
                elif o["fn"] is not None:
                    ins = o["fn"](e)
                    if o["sig"]:
                        ins.then_inc(sems["eng_" + ename], 1)
                else:
                    assert not o["sig"]

        for ename in self.ENGS:
            if not self.ops[ename]:
                continue
            getattr(block, engobj[ename])(lambda e, ename=ename: emit_engine(ename, e))


class Prog:
    def __init__(self, steps, layers, dbg=None):
        self.steps = steps
        self.layers = layers
        self.dbg = dbg or {}

    def declare(self, nc):
        d = {}

        def inp(name, shape):
            d[name] = nc.dram_tensor(name, list(shape), F32, kind="ExternalInput").ap()

        def outp(name, shape):
            d[name] = nc.dram_tensor(name, list(shape), F32, kind="ExternalOutput").ap()

        kinds = [k for k, _ in self.steps]
        inp("x_in", (NCH, 128, TOK))
        outp("x_out", (NCH, 128, TOK))
        inp("ones", (128, 128))
        for l in self.layers:
            for w in (1, 2):
                if (f"ffn{w}", l) in self.steps:
                    inp(f"wg{w}_{l}", (NM, 128, NCH, 128))
                    inp(f"wu{w}_{l}", (NM, 128, NCH, 128))
                    inp(f"wd{w}_{l}", (NCH, 128, NM, 128))
                    inp(f"gn{w}_{l}", (128, NCH))
            if ("mixA", l) in self.steps:
                inp(f"wq_{l}", (128, 4, NCH, 128))
                inp(f"wk_{l}", (128, 4, NCH, 128))
                inp(f"wsu_{l}", (128, 4, NCH, 128))
                inp(f"wv_{l}", (128, NCH, 512))
                inp(f"wsg_{l}", (128, NCH, 512))
                inp(f"gnm_{l}", (128, NCH))
                inp(f"gsg_{l}", (128, NCH))
                inp(f"lng_{l}", (512,))
                inp(f"lnb_{l}", (512,))
                inp(f"wsT_{l}", (128, 4, 128))
                inp(f"bsb_{l}", (4, 512))
            if ("mixB", l) in self.steps:
                inp(f"wo_{l}", (128, NCH, NCH, 128))
                inp(f"gat_{l}", (128, NCH))
        def scratch(name, shape):
            return nc.dram_tensor(name, list(shape), F32)

        if "mixA" in kinds:
            inp("rotPT", (128, 128))
            inp("cosT", (128, TOK))
            inp("sinT", (128, TOK))
            inp("masks", (128, 512))
            d["qT"] = scratch("qT_s", (4, 128, TOK)).ap()
            d["sgn"] = scratch("sgn_s", (4, 128, TOK)).ap()
            d["kTh"] = scratch("kTh_s", (4, 128, KTOK)).ap()
            d["vh"] = scratch("vh_s", (KTOK, 512)).ap()
            self.kv_own = {}
            self.kv_all = {}
            for l in self.layers:
                self.kv_own[l] = scratch(f"kv_own{l}", (2 * TOK, 512))
                self.kv_all[l] = scratch(f"kv_all{l}", (NCORES * 2 * TOK, 512))
            if self.dbg.get("a_out"):
                outp("a_out", (4, 128, TOK))
        if self.dbg.get("dump_scratch"):
            outp("qT_dump", (4, 128, TOK))
            outp("sgn_dump", (4, 128, TOK))
        if self.dbg.get("dump_halo"):
            outp("kTh_dump", (4, 128, KTOK))
            outp("vh_dump", (KTOK, 512))
        if "final" in kinds:
            inp("gfinal", (128, NCH))
        self.d = d
        return d

    def build(self):
        from contextlib import ExitStack
        nc = bass.Bass("TRN2", target_bir_lowering=False)
        d = self.declare(nc)
        S = Sched(nc)
        self.S = S
        self.db = {k: Buf("dram_" + k, multi=True) for k in ("qT", "sgn", "kTh", "vh")}
        for l in getattr(self, "kv_own", {}):
            self.db[f"own{l}"] = Buf(f"dram_own{l}", multi=True)
            self.db[f"all{l}"] = Buf(f"dram_all{l}", multi=True)
        with ExitStack() as st:
            def sb(name, cols):
                return st.enter_context(nc.sbuf_tensor("sb_" + name, [128, cols], F32))

            xT = [sb(f"xT{c}", TOK) for c in range(NCH)]
            self.x = [[Buf(f"x{c}_{b}", xT[c][:, b * BLK:(b + 1) * BLK]) for b in range(NB)] for c in range(NCH)]
            hTt = sb("hT", NCH * BLK)
            self.h = [Buf(f"h{c}", hTt[:, c * BLK:(c + 1) * BLK]) for c in range(NCH)]
            ones_t = sb("ones", 128)
            self.ones = Buf("ones", ones_t[:])
            sq_t = sb("sq", 3 * BLK)
            self.sq = [Buf(f"sq{i}", sq_t[:, i * BLK:(i + 1) * BLK]) for i in range(3)]
            rstd_t = sb("rstd", BLK)
            self.rstd = Buf("rstd", rstd_t[:])
            gains_t = sb("gains", 64 * NCH)
            self.gains_t = gains_t
            self.gain_slot = {}
            self.arena = sb("arena", ARENA)
            self.ps = []
            for i in range(8):
                p = st.enter_context(nc.psum_tensor(f"ps{i}", [128, BLK], F32))
                self.ps.append(Buf(f"ps{i}", p[:]))

            S.dma("sp", "ld_misc", ones_t[:], d["ones"], writes=[self.ones])
            for c in range(NCH):
                S.dma("sp", "ld_x", xT[c][:], d["x_in"][c], writes=self.x[c])
            self.load_gains()

            for kind, l in self.steps:
                if kind in ("ffn1", "ffn2"):
                    self.ffn(l, int(kind[-1]))
                elif kind == "mixA":
                    self.mixA1(l)
                    S.barrier()
                    self.exchange_start(l)
                    self.mixA2(l)
                    S.barrier()
                    self.exchange_finish(l)
                    if self.dbg.get("dump_halo"):
                        S.dma("sp", "ld_x", d["kTh_dump"], d["kTh"], reads=[self.db["kTh"]])
                        S.dma("sp", "ld_x", d["vh_dump"], d["vh"], reads=[self.db["vh"]])
                elif kind == "mixB":
                    self.mixB(l)
                elif kind == "final":
                    self.final_norm()
                else:
                    raise ValueError(kind)
                S.barrier()

            for c in range(NCH):
                S.dma("pool", "st_x", d["x_out"][c], xT[c][:], reads=self.x[c])
            S.wait_all_dma_on("pool")

            semkeys = ["eng_" + e for e in Sched.ENGS] + sorted(S.dma_count.keys())
            sems = {k: st.enter_context(nc.semaphore(k)) for k in semkeys}
            with nc.Block() as block:
                S.emit(block, sems)
        return nc

    def load_gains(self):
        S, d = self.S, self.d
        i = 0
        for name in sorted(d.keys()):
            if name.startswith(("gn", "gsg", "gat")) or name == "gfinal":
                ap = self.gains_t[:, i * NCH:(i + 1) * NCH]
                b = Buf(name, ap)
                S.dma("sp", "ld_misc", ap, d[name], writes=[b])
                self.gain_slot[name] = b
                i += 1

    def norm_block(self, b, gain, out_bufs=None, src=None, nfeat=D):
        S = self.S
        src = src if src is not None else [self.x[c][b] for c in range(NCH)]
        out_bufs = out_bufs if out_bufs is not None else self.h
        n = len(src)
        ss = self.ps[6]
        acc = self.sq[2]
        for c in range(n):
            sq = acc if c == 0 else self.sq[c % 2]
            S.op("act", lambda e, o=sq.ap, i=src[c].ap: e.activation(out=o, in_=i, func=AF.Square),
                 reads=[src[c]], writes=[sq])
            if c > 0:
                S.op("dve", lambda e, o=acc.ap, i=sq.ap: e.tensor_tensor(o, o, i, ALU.add),
                     reads=[acc, sq], writes=[acc])
        S.op("pe", lambda e, o=ss.ap, w=self.ones.ap, i=acc.ap: e.matmul(o, w, i, start=True, stop=True),
             reads=[self.ones, acc], writes=[ss])
        S.op("act", lambda e, o=self.rstd.ap, i=ss.ap: e.activation(out=o, in_=i, func=AF.Sqrt,
                                                                     scale=1.0 / nfeat, bias=EPS),
             reads=[ss], writes=[self.rstd])
        S.op("dve", lambda e, o=self.rstd.ap: e.reciprocal(o, o), reads=[self.rstd], writes=[self.rstd])
        for c in range(n):
            S.op("dve", lambda e, o=out_bufs[c].ap, i=src[c].ap, g=gain.ap[:, c:c + 1], r=self.rstd.ap:
                 e.scalar_tensor_tensor(o, i, g, r, ALU.mult, ALU.mult),
                 reads=[src[c], gain, self.rstd], writes=[out_bufs[c]])

    def ffn(self, l, w):
        S, d = self.S, self.d
        A = self.arena
        off = 0
        hid = [Buf(f"hid{m}", A[:, off + m * BLK: off + (m + 1) * BLK]) for m in range(NM)]
        off += NM * BLK
        NGU, NDN = 3, 2
        wgu = []
        for s in range(NGU):
            wgu.append((Buf(f"wg_s{s}", A[:, off:off + NCH * 128]),
                        Buf(f"wu_s{s}", A[:, off + NCH * 128: off + 2 * NCH * 128])))
            off += 2 * NCH * 128
        wdn = []
        for s in range(NDN):
            wdn.append(Buf(f"wd_s{s}", A[:, off:off + NM * 128]))
            off += NM * 128
        tmp = [Buf(f"silu{s}", A[:, off + s * BLK: off + (s + 1) * BLK]) for s in range(2)]
        off += 2 * BLK
        assert off <= ARENA, off
        gain = self.gain_slot[f"gn{w}_{l}"]
        WG, WU, WD = d[f"wg{w}_{l}"], d[f"wu{w}_{l}"], d[f"wd{w}_{l}"]

        def load_gu(g):
            if g >= NB * NM:
                return
            m = g % NM
            bg, bu = wgu[g % NGU]
            S.dma("sp", "ld_w", bg.ap.rearrange("p (k c) -> p k c", k=NCH), WG[m], writes=[bg])
            S.dma("sp", "ld_w", bu.ap.rearrange("p (k c) -> p k c", k=NCH), WU[m], writes=[bu])

        def load_dn(q):
            if q >= NB * NCH:
                return
            dc = q % NCH
            bd = wdn[q % NDN]
            S.dma("sp", "ld_w", bd.ap.rearrange("p (m c) -> p m c", m=NM), WD[dc], writes=[bd])

        for g in range(NGU):
            load_gu(g)
        for q in range(NDN):
            load_dn(q)

        for b in range(NB):
            self.norm_block(b, gain)
            for m in range(NM):
                g = b * NM + m
                bg, bu = wgu[g % NGU]
                pg, pu = self.ps[m % 2], self.ps[2 + m % 2]

                def mm(e, o, wt):
                    ins = None
                    for k in range(NCH):
                        ins = e.matmul(o, wt[:, k * 128:(k + 1) * 128], self.h[k].ap,
                                       start=(k == 0), stop=(k == NCH - 1))
                    return ins
                S.op("pe", lambda e, o=pg.ap, wt=bg.ap: mm(e, o, wt), reads=[bg] + self.h, writes=[pg])
                S.op("pe", lambda e, o=pu.ap, wt=bu.ap: mm(e, o, wt), reads=[bu] + self.h, writes=[pu])
                t = tmp[m % 2]
                S.op("act", lambda e, o=t.ap, i=pg.ap: e.activation(out=o, in_=i, func=AF.Silu),
                     reads=[pg], writes=[t])
                S.op("dve", lambda e, o=hid[m].ap, a=t.ap, bb=pu.ap: e.tensor_tensor(o, a, bb, ALU.mult),
                     reads=[t, pu], writes=[hid[m]])
                load_gu(g + NGU)
            for dc in range(NCH):
                q = b * NCH + dc
                bd = wdn[q % NDN]
                po = self.ps[4 + dc % 2]

                def mmd(e, o, wt):
                    ins = None
                    for m in range(NM):
                        ins = e.matmul(o, wt[:, m * 128:(m + 1) * 128], hid[m].ap,
                                       start=(m == 0), stop=(m == NM - 1))
                    return ins
                S.op("pe", lambda e, o=po.ap, wt=bd.ap: mmd(e, o, wt), reads=[bd] + hid, writes=[po])
                xb = self.x[dc][b]
                S.op("dve", lambda e, o=xb.ap, p=po.ap: e.scalar_tensor_tensor(o, p, 0.5, o, ALU.mult, ALU.add),
                     reads=[po, xb], writes=[xb])
                load_dn(q + NDN)


    def mixA1(self, l):
        S, d = self.S, self.d
        A = self.arena
        off = 0

        def carve(name, cols):
            nonlocal off
            b = Buf(name, A[:, off:off + cols])
            off += cols
            return b
        wq = carve("wq", 4 * NCH * 128)
        wk = carve("wk", 4 * NCH * 128)
        wv = carve("wv", NCH * 512)
        cosT = carve("cosT", TOK)
        sinT = carve("sinT", TOK)
        rotPT = carve("rotPT", 128)
        ksb = [carve(f"ksb{i}", BLK) for i in range(2)]
        t1 = [carve(f"rt1_{i}", BLK) for i in range(2)]
        t2 = [carve(f"rt2_{i}", BLK) for i in range(2)]
        vsb = [carve(f"vsb{i}", 512) for i in range(2)]
        assert off <= ARENA, off
        gain = self.gain_slot[f"gnm_{l}"]
        S.dma("sp", "ld_w", wk.ap.rearrange("p (m k c) -> p m k c", m=4, k=NCH), d[f"wk_{l}"], writes=[wk])
        S.dma("sp", "ld_misc", rotPT.ap, d["rotPT"], writes=[rotPT])
        S.dma("sp", "ld_misc", cosT.ap, d["cosT"], writes=[cosT])
        S.dma("sp", "ld_misc", sinT.ap, d["sinT"], writes=[sinT])
        S.dma("sp", "ld_w", wq.ap.rearrange("p (m k c) -> p m k c", m=4, k=NCH), d[f"wq_{l}"], writes=[wq])
        S.dma("sp", "ld_w", wv.ap.rearrange("p (k n) -> p k n", k=NCH), d[f"wv_{l}"], writes=[wv])
        cnt = 0
        for b in range(NB):
            self.norm_block(b, gain)
            bs = slice(b * BLK, (b + 1) * BLK)
            kvK = self.kv_own[l].ap()[TOK:2 * TOK, :].rearrange("(h j p r) c -> h j p r c", h=2, j=4, p=128, r=2)
            kvV = self.kv_own[l].ap()[0:TOK, :]
            for (wt, isk) in ((wk, True), (wq, False)):
                for mc in range(4):
                    pp = self.ps[cnt % 2]
                    pr = self.ps[2 + cnt % 2]
                    kb, a1, a2 = ksb[cnt % 2], t1[cnt % 2], t2[cnt % 2]
                    cnt += 1

                    def mm(e, o, w, mc=mc):
                        ins = None
                        for k in range(NCH):
                            c0 = (mc * NCH + k) * 128
                            ins = e.matmul(o, w[:, c0:c0 + 128], self.h[k].ap, start=(k == 0), stop=(k == NCH - 1))
                        return ins
                    S.op("pe", lambda e, o=pp.ap, w=wt.ap, mm=mm: mm(e, o, w), reads=[wt] + self.h, writes=[pp])
                    S.op("act", lambda e, o=kb.ap, i=pp.ap: e.copy(o, i), reads=[pp], writes=[kb])
                    S.op("pe", lambda e, o=pr.ap, w=rotPT.ap, i=kb.ap: e.matmul(o, w, i, start=True, stop=True),
                         reads=[rotPT, kb], writes=[pr])
                    S.op("dve", lambda e, o=a1.ap, i=kb.ap, c=cosT.ap[:, bs]: e.tensor_tensor(o, i, c, ALU.mult),
                         reads=[kb, cosT], writes=[a1])
                    S.op("dve", lambda e, o=a2.ap, i=pr.ap, c=sinT.ap[:, bs]: e.tensor_tensor(o, i, c, ALU.mult),
                         reads=[pr, sinT], writes=[a2])
                    S.op("pool", lambda e, o=a1.ap, i=a2.ap: e.tensor_tensor(o, o, i, ALU.add),
                         reads=[a1, a2], writes=[a1])
                    if isk:
                        S.dma("pool", "st_a", d["kTh"][mc][:, HALO + b * BLK: HALO + (b + 1) * BLK], a1.ap,
                              reads=[a1], writes=[self.db["kTh"]])
                        S.dma("pool", "st_a", kvK[b // 2, mc, :, b % 2, :], a1.ap, reads=[a1], writes=[self.db[f"own{l}"]])
                    else:
                        S.dma("pool", "st_a", d["qT"][mc][:, bs], a1.ap, reads=[a1], writes=[self.db["qT"]])
            for tt in range(4):
                pv = self.ps[4 + tt % 2]
                vb = vsb[tt % 2]
                ts_ = slice(tt * 128, (tt + 1) * 128)

                def mmv(e, o, w, ts_=ts_):
                    ins = None
                    for k in range(NCH):
                        ins = e.matmul(o, self.h[k].ap[:, ts_], w[:, k * 512:(k + 1) * 512],
                                       start=(k == 0), stop=(k == NCH - 1))
                    return ins
                S.op("pe", lambda e, o=pv.ap, w=wv.ap, mmv=mmv: mmv(e, o, w), reads=[wv] + self.h, writes=[pv])
                S.op("act", lambda e, o=vb.ap, i=pv.ap: e.copy(o, i), reads=[pv], writes=[vb])
                r0 = b * BLK + tt * 128
                S.dma("pool", "st_a", d["vh"][HALO + r0:HALO + r0 + 128, :], vb.ap, reads=[vb], writes=[self.db["vh"]])
                S.dma("pool", "st_a", kvV[r0:r0 + 128, :], vb.ap, reads=[vb], writes=[self.db[f"own{l}"]])

    def mixA2(self, l):
        S, d = self.S, self.d
        A = self.arena
        off = 0

        def carve(name, cols):
            nonlocal off
            b = Buf(name, A[:, off:off + cols])
            off += cols
            return b
        wu = carve("wu2", 4 * NCH * 128)
        wg = carve("wg2", NCH * 512)
        lng = carve("lng", 512)
        lnb = carve("lnb", 512)
        bsb = [carve(f"bsb{g}", 512) for g in range(4)]
        wsT = carve("wsT", 4 * 128)
        gel = [carve(f"gel{i}", 512) for i in range(2)]
        vn = [carve(f"vn{i}", 512) for i in range(2)]
        junk = carve("junk", 512)
        st_ = [carve(f"stat{i}", 8) for i in range(2)]
        gu = [carve(f"gu{g}", BLK) for g in range(4)]
        sg = [carve(f"sg{g}", BLK) for g in range(4)]
        sgn = [carve(f"sgn{g}", BLK) for g in range(4)]
        assert off <= ARENA, off
        gain = self.gain_slot[f"gnm_{l}"]
        gsg = self.gain_slot[f"gsg_{l}"]
        S.dma("sp", "ld_w", wg.ap.rearrange("p (k n) -> p k n", k=NCH), d[f"wsg_{l}"], writes=[wg])
        S.dma("sp", "ld_misc", lng.ap, d[f"lng_{l}"].partition_broadcast(128), writes=[lng])
        S.dma("sp", "ld_misc", lnb.ap, d[f"lnb_{l}"].partition_broadcast(128), writes=[lnb])
        S.dma("sp", "ld_misc", wsT.ap.rearrange("p (g t) -> p g t", g=4), d[f"wsT_{l}"], writes=[wsT])
        for g in range(4):
            S.dma("sp", "ld_misc", bsb[g].ap, d[f"bsb_{l}"][g].partition_broadcast(128), writes=[bsb[g]])
        S.dma("sp", "ld_w", wu.ap.rearrange("p (m k c) -> p m k c", m=4, k=NCH), d[f"wsu_{l}"], writes=[wu])
        for b in range(NB):
            self.norm_block(b, gain)
            bs = slice(b * BLK, (b + 1) * BLK)
            pmix = self.ps[0:4]
            for tt in range(4):
                pg = self.ps[4 + tt % 2]
                gl, vv, sx = gel[tt % 2], vn[tt % 2], st_[tt % 2]
                ts_ = slice(tt * 128, (tt + 1) * 128)

                def mmg(e, o, w, ts_=ts_):
                    ins = None
                    for k in range(NCH):
                        ins = e.matmul(o, self.h[k].ap[:, ts_], w[:, k * 512:(k + 1) * 512],
                                       start=(k == 0), stop=(k == NCH - 1))
                    return ins
                S.op("pe", lambda e, o=pg.ap, w=wg.ap, mmg=mmg: mmg(e, o, w), reads=[wg] + self.h, writes=[pg])
                S.op("dve", lambda e, a=sx.ap: e.memset(a, 0.0), writes=[sx])
                S.op("act", lambda e, o=gl.ap, i=pg.ap, a=sx.ap[:, 0:1]: e.activation(out=o, in_=i, func=AF.Gelu, accum_out=a),
                     reads=[pg], writes=[gl, sx])
                S.op("dve", lambda e, a=sx.ap: e.tensor_scalar(a[:, 1:2], a[:, 0:1], 1.0 / 512, None, ALU.mult),
                     reads=[sx], writes=[sx])
                S.op("dve", lambda e, o=gl.ap, a=sx.ap: e.tensor_scalar(o, o, a[:, 1:2], None, ALU.subtract),
                     reads=[gl, sx], writes=[gl])
                S.op("act", lambda e, o=junk.ap, i=gl.ap, a=sx.ap[:, 2:3]: e.activation(out=o, in_=i, func=AF.Square, accum_out=a),
                     reads=[gl], writes=[junk, sx])
                S.op("act", lambda e, a=sx.ap: e.activation(out=a[:, 3:4], in_=a[:, 2:3], func=AF.Sqrt, scale=1.0 / 512, bias=EPS),
                     reads=[sx], writes=[sx])
                S.op("dve", lambda e, a=sx.ap: e.reciprocal(a[:, 4:5], a[:, 3:4]), reads=[sx], writes=[sx])
                S.op("dve", lambda e, o=vv.ap, i=gl.ap, a=sx.ap, g_=lng.ap: e.scalar_tensor_tensor(o, i, a[:, 4:5], g_, ALU.mult, ALU.mult),
                     reads=[gl, sx, lng], writes=[vv])
                S.op("dve", lambda e, o=vv.ap, b_=lnb.ap: e.tensor_tensor(o, o, b_, ALU.add),
                     reads=[vv, lnb], writes=[vv])
                for g in range(4):
                    S.op("pe", lambda e, o=pmix[g].ap[:, ts_], w=vv.ap[:, g * 128:(g + 1) * 128], r=wsT.ap[:, g * 128:(g + 1) * 128]:
                         e.matmul(o, w, r, start=True, stop=True),
                         reads=[vv, wsT], writes=[pmix[g]])
            for g in range(4):
                pu = self.ps[4 + g % 2]

                def mmu(e, o, w, g=g):
                    ins = None
                    for k in range(NCH):
                        c0 = (g * NCH + k) * 128
                        ins = e.matmul(o, w[:, c0:c0 + 128], self.h[k].ap, start=(k == 0), stop=(k == NCH - 1))
                    return ins
                S.op("pe", lambda e, o=pu.ap, w=wu.ap, mmu=mmu: mmu(e, o, w), reads=[wu] + self.h, writes=[pu])
                S.op("act", lambda e, o=gu[g].ap, i=pu.ap: e.activation(out=o, in_=i, func=AF.Gelu),
                     reads=[pu], writes=[gu[g]])
                S.op("dve", lambda e, o=sg[g].ap, i=pmix[g].ap, b_=bsb[g].ap: e.tensor_tensor(o, i, b_, ALU.add),
                     reads=[pmix[g], bsb[g]], writes=[sg[g]])
                S.op("dve", lambda e, o=sg[g].ap, i=gu[g].ap: e.tensor_tensor(o, o, i, ALU.mult),
                     reads=[sg[g], gu[g]], writes=[sg[g]])
            self.norm_block(b, gsg, out_bufs=sgn, src=sg, nfeat=512)
            for g in range(4):
                S.dma("pool", "st_a", d["sgn"][g][:, bs], sgn[g].ap, reads=[sgn[g]], writes=[self.db["sgn"]])


    def exchange_start(self, l):
        S = self.S
        own, allg = self.kv_own[l], self.kv_all[l]
        S.coll("cc", lambda e: e.collective_compute("AllGather", ALU.bypass, replica_groups=[list(range(NCORES))],
                                                    ins=[own.ap().opt()], outs=[allg.ap().opt()]),
               reads=[self.db[f"own{l}"]], writes=[self.db[f"all{l}"]])
        self.S.fence["pool"].add(self.S.dma_last["cc"])
        self.S.op("pool", None)

    def exchange_finish(self, l):
        S, d = self.S, self.d
        allg = self.kv_all[l].ap()
        RB = 2 * TOK
        kth, vh = d["kTh"], d["vh"]

        def pid(e):
            if getattr(self, "_pid", None) is None:
                self._pid = e.partition_id()
            return self._pid

        def prev(e):
            return (pid(e) + (NCORES - 1)) % NCORES

        def nxt(e):
            return (pid(e) + 1) % NCORES
        rd, wr = [self.db[f"all{l}"]], None
        S.dma("sp", "ld_x", vh[0:HALO, :], lambda e: allg[bass.ds(prev(e) * RB + HALO, HALO), :],
              reads=rd, writes=[self.db["vh"]])
        S.dma("sp", "ld_x", vh[HALO + TOK:KTOK, :], lambda e: allg[bass.ds(nxt(e) * RB, HALO), :],
              reads=rd, writes=[self.db["vh"]])
        S.dma("sp", "ld_x", kth[:, :, 0:HALO].rearrange("j p (r c) -> j p r c", c=512),
              lambda e: allg[bass.ds(prev(e) * RB + TOK + HALO, HALO), :].rearrange("(j p r) c -> j p r c", j=4, p=128, r=2),
              reads=rd, writes=[self.db["kTh"]])
        S.dma("sp", "ld_x", kth[:, :, HALO + TOK:KTOK].rearrange("j p (r c) -> j p r c", c=512),
              lambda e: allg[bass.ds(nxt(e) * RB + TOK, HALO), :].rearrange("(j p r) c -> j p r c", j=4, p=128, r=2),
              reads=rd, writes=[self.db["kTh"]])

    def mixB(self, l):
        S, d = self.S, self.d
        A = self.arena
        off = 0

        def carve(name, cols):
            nonlocal off
            b = Buf(name, A[:, off:off + cols])
            off += cols
            return b
        a_sb = [carve(f"a_sb{j}", TOK) for j in range(4)]
        off_attn = off
        qT = carve("qT", TOK)
        kT = carve("kTh", KTOK)
        VT = 17 * 128
        vsl = [carve(f"vt{i}", VT) for i in range(3)]
        uz = carve("uzacc", 2 * TOK)
        esb = [carve(f"esb{i}", 256) for i in range(3)]
        masks = carve("masks", 512)
        assert off <= ARENA, off
        uz3 = uz.ap.rearrange("p (u t) -> p u t", u=2)
        S.dma("sp", "ld_misc", masks.ap, d["masks"], writes=[masks])
        vcnt = 0
        ecnt = 0
        for j in self.dbg.get("pairs", range(4)):
            S.dma("sp", "ld_a", qT.ap, d["qT"][j], reads=[self.db["qT"]], writes=[qT])
            S.dma("sp", "ld_a", kT.ap, d["kTh"][j], reads=[self.db["kTh"]], writes=[kT])
            pats = self.dbg.get("patterns", PATTERNS)
            groups = [(d_, r) for d_ in pats for r in range(d_)]
            units = []
            for g, (d_, r) in enumerate(groups):
                nq = TOK // d_ // 128
                for i in range(nq + 1):
                    qlo, qhi = max(i - 1, 0), min(i, nq - 1)
                    if i == 0:
                        mk = masks.ap[:, 256:384]
                    elif i == nq:
                        mk = masks.ap[:, 384:512]
                    else:
                        mk = masks.ap[:, 0:256]
                    for hh in range(2):
                        units.append(dict(g=g, d=d_, r=r, nq=nq, base=HALO - 64 * d_, i=i, hh=hh, qlo=qlo, qhi=qhi,
                                          N=128 * (qhi - qlo + 1), mk=mk))
            nu = len(units)
            vbase, ebase = vcnt, ecnt

            def load_v(g):
                d_, r = groups[g]
                nq = TOK // d_ // 128
                vt = vsl[(vbase + g) % 3]
                nrow = (nq + 1) * 128
                r0 = HALO - 64 * d_ + r
                src = d["vh"][r0:r0 + d_ * (nrow - 1) + 1:d_, j * 128:(j + 1) * 128]
                vt3 = vt.ap[:, 0:(nq + 1) * 128].rearrange("p (i c) -> p i c", c=128)
                S.dma("sp", "ld_a", vt3, src.rearrange("(i p) c -> p i c", p=128), reads=[self.db["vh"]], writes=[vt])

            def emit_S(u):
                U = units[u]
                d_, r, i, hh, N = U["d"], U["r"], U["i"], U["hh"], U["N"]
                R = slice(hh * 64, hh * 64 + 64)
                pS = self.ps[(ebase + u) % 3]
                eb = esb[(ebase + u) % 3]
                k0 = U["base"] + r + d_ * 128 * i
                q0 = r + d_ * 128 * U["qlo"]
                kap = kT.ap[R, k0:k0 + d_ * 127 + 1:d_]
                qap = qT.ap[R, q0:q0 + d_ * (N - 1) + 1:d_]
                S.op("pe", lambda e, o=pS.ap[:, 0:N], w=kap, i_=qap: e.matmul(o, w, i_, start=True, stop=True),
                     reads=[kT, qT], writes=[pS])
                S.op("act", lambda e, o=eb.ap[:, 0:N], i_=pS.ap[:, 0:N]: e.activation(out=o, in_=i_, func=AF.Exp, scale=0.125),
                     reads=[pS], writes=[eb])
                S.op("pool", lambda e, o=eb.ap[:, 0:N], m=U["mk"]: e.tensor_tensor(o, o, m, ALU.mult),
                     reads=[eb, masks], writes=[eb])

            def emit_PV(u):
                U = units[u]
                d_, r, i, hh, qlo, qhi = U["d"], U["r"], U["i"], U["hh"], U["qlo"], U["qhi"]
                R = slice(hh * 64, hh * 64 + 64)
                eb = esb[(ebase + u) % 3]
                vt = vsl[(vbase + U["g"]) % 3]
                for qt in range(qlo, qhi + 1):
                    pz = self.ps[4 + qt % 2]
                    ecol = slice((qt - qlo) * 128, (qt - qlo + 1) * 128)
                    vap = vt.ap[:, i * 128 + hh * 64: i * 128 + hh * 64 + 64]
                    tp = (0, 64) if hh == 1 else None

                    def pv(e, pz=pz, R=R, vap=vap, eap=eb.ap[:, ecol], tp=tp, st=(i == qt), sp=(i == qt + 1)):
                        kw = {"tile_position": tp} if tp is not None else {}
                        e.matmul(pz.ap[R, 0:128], vap, eap, start=st, stop=sp, **kw)
                        return e.matmul(pz.ap[R, 128:256], self.ones.ap[:, 0:64], eap, start=False, stop=sp,
                                        skip_group_check=True, **kw)
                    S.op("pe", pv, reads=[vt, eb, self.ones], writes=[pz])
                if hh == 1 and i >= 1:
                    qt = i - 1
                    pz = self.ps[4 + qt % 2]
                    t0 = r + d_ * 128 * qt
                    dst = uz3[:, :, t0:t0 + d_ * 127 + 1:d_]
                    srcp = pz.ap[:, 0:256].rearrange("p (u t) -> p u t", u=2)
                    if d_ == pats[0]:
                        S.op("dve", lambda e, o=dst, i_=srcp: e.tensor_copy(o, i_), reads=[pz], writes=[uz])
                    else:
                        S.op("dve", lambda e, o=dst, i_=srcp: e.tensor_tensor(o, o, i_, ALU.add), reads=[pz, uz], writes=[uz])

            AHEAD = 2
            load_v(0)
            for u in range(min(AHEAD, nu)):
                emit_S(u)
            for u in range(nu):
                U = units[u]
                if U["i"] == 0 and U["hh"] == 0 and U["g"] + 1 < len(groups):
                    load_v(U["g"] + 1)
                if u + AHEAD < nu:
                    emit_S(u + AHEAD)
                emit_PV(u)
            vcnt += len(groups)
            ecnt += nu
            S.op("dve", lambda e, z=uz.ap[:, TOK:2 * TOK]: e.reciprocal(z, z), reads=[uz], writes=[uz])
            S.op("dve", lambda e, o=a_sb[j].ap, u=uz.ap[:, 0:TOK], z=uz.ap[:, TOK:2 * TOK]: e.tensor_tensor(o, u, z, ALU.mult),
                 reads=[uz], writes=[a_sb[j]])
            if self.dbg.get("a_out"):
                S.dma("pool", "st_a", d["a_out"][j], a_sb[j].ap, reads=[a_sb[j]])
        S.barrier()
        if self.dbg.get("dump_scratch"):
            S.dma("sp", "ld_x", d["qT_dump"], d["qT"], reads=[self.db["qT"]])
            S.dma("sp", "ld_x", d["sgn_dump"], d["sgn"], reads=[self.db["sgn"]])
        if self.dbg.get("no_wout"):
            return
        off = off_attn
        wo = carve("wo", NCH * NCH * 128)
        an = [carve(f"an{j}", BLK) for j in range(4)]
        sgb = [carve(f"sgb{j}", BLK) for j in range(4)]
        assert off <= ARENA, off
        gat = self.gain_slot[f"gat_{l}"]
        S.dma("sp", "ld_w", wo.ap.rearrange("p (a k c) -> p a k c", a=NCH, k=NCH), d[f"wo_{l}"], writes=[wo])
        for b in range(NB):
            bs = slice(b * BLK, (b + 1) * BLK)
            for g in range(4):
                S.dma("sp", "ld_a", sgb[g].ap, d["sgn"][g][:, bs], reads=[self.db["sgn"]], writes=[sgb[g]])
            srcs = [Buf(f"a{j}_{b}", a_sb[j].ap[:, bs]) for j in range(4)]
            self.norm_block(b, gat, out_bufs=an, src=srcs, nfeat=512)
            rhs = an + sgb
            for dc in range(NCH):
                po = self.ps[dc % 2]

                def mmo(e, o, dc=dc):
                    ins = None
                    for k in range(NCH):
                        c0 = (dc * NCH + k) * 128
                        ins = e.matmul(o, wo.ap[:, c0:c0 + 128], rhs[k].ap, start=(k == 0), stop=(k == NCH - 1))
                    return ins
                S.op("pe", lambda e, o=po.ap, mmo=mmo: mmo(e, o), reads=[wo] + rhs, writes=[po])
                xb = self.x[dc][b]
                S.op("dve", lambda e, o=xb.ap, p=po.ap: e.tensor_tensor(o, o, p, ALU.add), reads=[po, xb], writes=[xb])

    def final_norm(self):
        gain = self.gain_slot["gfinal"]
        for b in range(NB):
            self.norm_block(b, gain, out_bufs=[self.x[c][b] for c in range(NCH)])


def _feat_major(v, n):
    return np.ascontiguousarray(np.asarray(v, np.float32).reshape(n, 128).T)


def _w_stationary(W):
    K, M = W.shape
    return np.ascontiguousarray(W.reshape(K // 128, 128, M // 128, 128).transpose(2, 1, 0, 3))


def _x_to_cores(x):
    outs = []
    for core in range(NCORES):
        b, q = divmod(core, 4)
        xs = x[b, q * TOK:(q + 1) * TOK, :]
        outs.append(np.ascontiguousarray(xs.T.reshape(NCH, 128, TOK)))
    return outs


def _cores_to_x(xs):
    out = np.empty((2, 8192, D), np.float32)
    for core in range(NCORES):
        b, q = divmod(core, 4)
        out[b, q * TOK:(q + 1) * TOK, :] = xs[core].reshape(D, TOK).T
    return out


def _w_stat_p_major(W):
    K, M = W.shape
    return np.ascontiguousarray(W.reshape(K // 128, 128, M // 128, 128).transpose(1, 2, 0, 3))


def _w_moving(W):
    K, N = W.shape
    return np.ascontiguousarray(W.reshape(K // 128, 128, N).transpose(1, 0, 2))


def _pad8(a):
    out = np.zeros((128, NCH), np.float32)
    out[:, :a.shape[1]] = a
    return out


def _const_tables():
    inv_freq = (np.float32(500000.0) ** (-np.arange(0, 16, 2, dtype=np.float32) / np.float32(16))).astype(np.float32)
    cos_q, sin_q = [], []
    for q in range(4):
        pos = np.arange(q * TOK, (q + 1) * TOK, dtype=np.float32)
        ang = (pos[:, None] * inv_freq[None, :]).astype(np.float32)
        c8, s8 = np.cos(ang).astype(np.float32), np.sin(ang).astype(np.float32)
        C = np.ones((128, TOK), np.float32)
        Sn = np.zeros((128, TOK), np.float32)
        for hb in (0, 64):
            for i in range(8):
                C[hb + i] = c8[:, i]
                C[hb + 8 + i] = c8[:, i]
                Sn[hb + i] = s8[:, i]
                Sn[hb + 8 + i] = s8[:, i]
        cos_q.append(C)
        sin_q.append(Sn)
    rotPT = np.zeros((128, 128), np.float32)
    for hb in (0, 64):
        for i in range(8):
            rotPT[hb + i + 8, hb + i] = -1.0
            rotPT[hb + i, hb + i + 8] = 1.0
    p = np.arange(128)[:, None]
    f = np.arange(128)[None, :]
    Bm = (f >= p).astype(np.float32)
    Am = (f <= p).astype(np.float32)
    masks = []
    for q in range(4):
        has_prev = 1.0 if q > 0 else 0.0
        has_next = 1.0 if q < 3 else 0.0
        first = Am * np.where(p >= 64, 1.0, has_prev)
        last = Bm * np.where(p < 64, 1.0, has_next)
        masks.append(np.ascontiguousarray(np.concatenate([Bm, Am, first, last], axis=1).astype(np.float32)))
    return cos_q, sin_q, rotPT, masks


_PROG_CACHE = {}


def _get_prog(steps):
    key = tuple(steps)
    if key not in _PROG_CACHE:
        P = Prog(list(steps), sorted({l for k, l in steps if k != "final"}))
        _PROG_CACHE[key] = P.build()
    return _PROG_CACHE[key]


def kernel(x, norm_ffn1, ffn1_w_gate, ffn1_w_up, ffn1_w_down, norm_mix, w_in,
           sgu_ln_g, sgu_ln_b, sgu_w, sgu_b, out_norm_attn, out_norm_sgu, w_out,
           norm_ffn2, ffn2_w_gate, ffn2_w_up, ffn2_w_down, final_norm):
    f32 = lambda a: np.asarray(a, np.float32)
    x = f32(x)
    cos_q, sin_q, rotPT, masks = _const_tables()
    ones = np.ones((128, 128), np.float32)

    def layer_inputs(l, slot, kinds):
        m = {}
        if "ffn1" in kinds:
            m[f"wg1_{slot}"] = _w_stationary(f32(ffn1_w_gate[l]))
            m[f"wu1_{slot}"] = _w_stationary(f32(ffn1_w_up[l]))
            m[f"wd1_{slot}"] = _w_stationary(f32(ffn1_w_down[l]))
            m[f"gn1_{slot}"] = _feat_major(norm_ffn1[l], NCH)
        if "ffn2" in kinds:
            m[f"wg2_{slot}"] = _w_stationary(f32(ffn2_w_gate[l]))
            m[f"wu2_{slot}"] = _w_stationary(f32(ffn2_w_up[l]))
            m[f"wd2_{slot}"] = _w_stationary(f32(ffn2_w_down[l]))
            m[f"gn2_{slot}"] = _feat_major(norm_ffn2[l], NCH)
        if "mixA" in kinds:
            wi = f32(w_in[l])
            m[f"wq_{slot}"] = _w_stat_p_major(wi[:, 0:512])
            m[f"wk_{slot}"] = _w_stat_p_major(wi[:, 512:1024])
            m[f"wv_{slot}"] = _w_moving(wi[:, 1024:1536])
            m[f"wsu_{slot}"] = _w_stat_p_major(wi[:, 1536:2048])
            m[f"wsg_{slot}"] = _w_moving(wi[:, 2048:2560])
            m[f"gnm_{slot}"] = _feat_major(norm_mix[l], NCH)
            m[f"gsg_{slot}"] = _pad8(_feat_major(out_norm_sgu[l], 4))
            m[f"lng_{slot}"] = np.ascontiguousarray(f32(sgu_ln_g[l]))
            m[f"lnb_{slot}"] = np.ascontiguousarray(f32(sgu_ln_b[l]))
            m[f"wsT_{slot}"] = np.ascontiguousarray(f32(sgu_w[l]).transpose(2, 0, 1))
            m[f"bsb_{slot}"] = np.ascontiguousarray(np.tile(f32(sgu_b[l]), (1, 4)))
        if "mixB" in kinds:
            m[f"wo_{slot}"] = _w_stat_p_major(f32(w_out[l]))
            m[f"gat_{slot}"] = _pad8(_feat_major(out_norm_attn[l], 4))
        return m

    xs = _x_to_cores(x)
    steps = []
    common = {"ones": ones, "rotPT": rotPT, "gfinal": _feat_major(final_norm, NCH)}
    for l in range(DEPTH):
        steps += [("ffn1", l), ("mixA", l), ("mixB", l), ("ffn2", l)]
        common.update(layer_inputs(l, l, ("ffn1", "mixA", "mixB", "ffn2")))
    steps += [("final", 0)]
    nc = _get_prog(steps)
    in_maps = []
    for core in range(NCORES):
        b, q = divmod(core, 4)
        m = dict(common)
        m["x_in"] = xs[core]
        m["cosT"] = cos_q[q]
        m["sinT"] = sin_q[q]
        m["masks"] = masks[q]
        in_maps.append(m)
    res = run_bass_kernel_spmd(nc, in_maps, core_ids=list(range(NCORES)))
    xs = [np.asarray(r["x_out"]) for r in res.results]
    return _cores_to_x(xs)
```
